# Optimizing a Trainium2 kernel written in Bass

```python
import jax, jax.numpy as jnp
from jax import lax
import numpy as np

D_MODEL = 2048
BATCH = 8
SEQ = 2048
DEPTH = 2

GRID_W = 64
CTX_LEN = 256
N_MIXERS = 2
N_RWKV = (DEPTH + 1) // 2
N_CONV = DEPTH // 2
HEAD_SIZE = 64
N_HEADS = D_MODEL // HEAD_SIZE
LORA_DECAY = max(32, int(round(1.8 * D_MODEL ** 0.5 / 32)) * 32)
LORA_A = max(32, int(round(1.8 * D_MODEL ** 0.5 / 32)) * 32)
LORA_GATE = max(32, int(round(0.6 * D_MODEL ** 0.8 / 32)) * 32)
CONV_WIDTH = 3
D_FF = ((8 * D_MODEL // 3 + 255) // 256) * 256
NORM_EPS = 1e-6
GN_EPS = 64e-5

kernel_name = "rwkv7_shortconv_hybrid_dit"


def rmsnorm(h, g):
    hf = h.astype(jnp.float32)
    hn = hf * lax.rsqrt(jnp.mean(hf * hf, axis=-1, keepdims=True) + NORM_EPS)
    return hn.astype(h.dtype) * g


def modulate(h, shift, scale):
    return h * (1 + scale) + shift


def split_heads(t):
    return t.reshape(t.shape[0], t.shape[1], N_HEADS, HEAD_SIZE)


def grid_shift(h):
    Bn, T, D = h.shape
    rows = T // GRID_W
    g = h.reshape(Bn, rows, GRID_W, D)
    q = D // 4
    left = jnp.pad(g[:, :, :-1, :q], ((0, 0), (0, 0), (1, 0), (0, 0)))
    right = jnp.pad(g[:, :, 1:, q:2 * q], ((0, 0), (0, 0), (0, 1), (0, 0)))
    up = jnp.pad(g[:, :-1, :, 2 * q:3 * q], ((0, 0), (1, 0), (0, 0), (0, 0)))
    down = jnp.pad(g[:, 1:, :, 3 * q:], ((0, 0), (0, 1), (0, 0), (0, 0)))
    return jnp.concatenate([left, right, up, down], axis=-1).reshape(Bn, T, D)


def seq_shift(h):
    half = h.shape[-1] // 2
    prev = jnp.pad(h[:, :-1, :half], ((0, 0), (1, 0), (0, 0)))
    nxt = jnp.pad(h[:, 1:, half:], ((0, 0), (0, 1), (0, 0)))
    return jnp.concatenate([prev, nxt], axis=-1)


def wkv_scan(state0, r, decay, k, v, a, b, reverse):
    def step(S, inp):
        r_t, w_t, k_t, v_t, a_t, b_t = inp
        sa = jnp.einsum('bhvk,bhk->bhv', S, a_t)
        S = (S * w_t[:, :, None, :] + sa[..., None] * b_t[:, :, None, :]
             + v_t[..., None] * k_t[:, :, None, :])
        return S, jnp.einsum('bhvk,bhk->bhv', S, r_t)
    xs = tuple(jnp.swapaxes(t.astype(jnp.float32), 0, 1) for t in (r, decay, k, v, a, b))
    state, ys = lax.scan(step, state0, xs, reverse=reverse)
    return state, jnp.swapaxes(ys, 0, 1)


def rwkv_shared(h, h_shift, mix, wr, wk, wv, g1, g2, k_k):
    xx = h_shift - h
    xr, xw, xk, xv, xa, xg = [h + xx * mix[m] for m in range(6)]
    r = split_heads(xr @ wr)
    k = xk @ wk
    v = split_heads(xv @ wv)
    g = jax.nn.sigmoid(xg @ g1) @ g2
    kk = split_heads((k * k_k).astype(jnp.float32))
    kk = kk / jnp.maximum(jnp.sqrt(jnp.sum(kk * kk, axis=-1, keepdims=True)), 1e-12)
    return r, k, v, kk, g, xw, xa


def rwkv_direction(xw, xa, k, kk, w0, w1, w2, a0, a1, a2, k_a):
    log_w = -jax.nn.softplus(-(w0 + jnp.tanh(xw @ w1) @ w2).astype(jnp.float32)) - 0.5
    decay = jnp.exp(-jnp.exp(log_w))
    a = jax.nn.sigmoid(a0 + (xa @ a1) @ a2)
    k_d = k * (1 + (a - 1) * k_a)
    a_h = split_heads(a).astype(jnp.float32)
    return split_heads(decay), split_heads(k_d), -kk, kk * a_h


def rwkv_readout(y, r, ksum, v, g, r_k, ln_w, ln_b, wo, dtype):
    Bn, T = y.shape[0], y.shape[1]
    mu = jnp.mean(y, axis=-1, keepdims=True)
    var = jnp.mean(jnp.square(y - mu), axis=-1, keepdims=True)
    o = ((y - mu) * lax.rsqrt(var + GN_EPS)).reshape(Bn, T, D_MODEL) * ln_w + ln_b
    bonus = jnp.sum(r * ksum * r_k, axis=-1, keepdims=True) * v
    o = o + bonus.reshape(Bn, T, D_MODEL)
    return (o.astype(dtype) * g) @ wo


def rwkv7_mix(hx, hc, need_ctx, mix, wr, wk, wv, wo, w0, w1, w2, a0, a1, a2,
              g1, g2, k_k, k_a, r_k, ln_w, ln_b):
    r_c, k_c, v_c, kk_c, g_c, xw_c, xa_c = rwkv_shared(hc, seq_shift(hc), mix, wr, wk, wv, g1, g2, k_k)
    r_x, k_x, v_x, kk_x, g_x, xw_x, xa_x = rwkv_shared(hx, grid_shift(hx), mix, wr, wk, wv, g1, g2, k_k)
    state0 = jnp.zeros((hx.shape[0], N_HEADS, HEAD_SIZE, HEAD_SIZE), jnp.float32)
    y_c = 0.0
    y_x = 0.0
    ksum_c = 0.0
    ksum_x = 0.0
    for d, rev in enumerate((False, True)):
        dec_c, kd_c, aa_c, bb_c = rwkv_direction(xw_c, xa_c, k_c, kk_c, w0[d], w1[d], w2[d], a0[d], a1[d], a2[d], k_a)
        state_c, yd_c = wkv_scan(state0, r_c, dec_c, kd_c, v_c, aa_c, bb_c, rev)
        dec_x, kd_x, aa_x, bb_x = rwkv_direction(xw_x, xa_x, k_x, kk_x, w0[d], w1[d], w2[d], a0[d], a1[d], a2[d], k_a)
        _, yd_x = wkv_scan(state_c, r_x, dec_x, kd_x, v_x, aa_x, bb_x, rev)
        y_x = y_x + yd_x
        ksum_x = ksum_x + kd_x
        if need_ctx:
            y_c = y_c + yd_c
            ksum_c = ksum_c + kd_c
    out_x = rwkv_readout(y_x, r_x, ksum_x, v_x, g_x, r_k, ln_w, ln_b, wo, hx.dtype)
    out_c = rwkv_readout(y_c, r_c, ksum_c, v_c, g_c, r_k, ln_w, ln_b, wo, hc.dtype) if need_ctx else None
    return out_x, out_c


def short_conv(h, w_in, conv_w, w_out):
    gb, gc, u = jnp.split(h @ w_in, 3, axis=-1)
    z = jnp.pad(gc * u, ((0, 0), (1, 1), (0, 0)))
    conv = z[:, :-2] * conv_w[0] + z[:, 1:-1] * conv_w[1] + z[:, 2:] * conv_w[2]
    return (gb * conv) @ w_out


def swiglu(h, w13, w2):
    a, b = jnp.split(h @ w13, 2, axis=-1)
    return (jax.nn.silu(a) * b) @ w2


def setup_inputs(seed: int = 0) -> dict:
    key = jax.random.key(seed)
    ks = iter(jax.random.split(key, 48))

    def nrm(shape, scale):
        return jax.random.normal(next(ks), shape, jnp.float32) * scale

    def unif(shape, lo, hi):
        return jax.random.uniform(next(ks), shape, jnp.float32, lo, hi)

    D, F = D_MODEL, D_FF
    inv = D ** -0.5
    return {
        "x": nrm((BATCH, SEQ, D), 1.0),
        "c": nrm((BATCH, D), 1.0),
        "ctx": nrm((BATCH, CTX_LEN, D), 1.0),
        "c_ctx": nrm((D,), 1.0),
        "norm1_g": 1.0 + nrm((DEPTH, D), 0.02),
        "norm2_g": 1.0 + nrm((DEPTH, D), 0.02),
        "ada_w": nrm((DEPTH, D, 6 * D), 0.5 * inv),
        "ada_b": nrm((DEPTH, 6 * D), 0.02),
        "rw_mix": unif((N_RWKV, 6, D), 0.0, 1.0),
        "rw_wr": nrm((N_RWKV, D, D), inv),
        "rw_wk": nrm((N_RWKV, D, D), inv),
        "rw_wv": nrm((N_RWKV, D, D), inv),
        "rw_wo": nrm((N_RWKV, D, D), inv),
        "rw_w0": unif((N_RWKV, 2, D), -6.5, -1.5),
        "rw_w1": nrm((N_RWKV, 2, D, LORA_DECAY), inv),
        "rw_w2": nrm((N_RWKV, 2, LORA_DECAY, D), 0.5 * LORA_DECAY ** -0.5),
        "rw_a0": nrm((N_RWKV, 2, D), 0.1),
        "rw_a1": nrm((N_RWKV, 2, D, LORA_A), inv),
        "rw_a2": nrm((N_RWKV, 2, LORA_A, D), 0.5 * LORA_A ** -0.5),
        "rw_g1": nrm((N_RWKV, D, LORA_GATE), inv),
        "rw_g2": nrm((N_RWKV, LORA_GATE, D), LORA_GATE ** -0.5),
        "rw_kk": 0.85 + nrm((N_RWKV, D), 0.05),
        "rw_ka": 1.0 + nrm((N_RWKV, D), 0.05),
        "rw_rk": nrm((N_RWKV, N_HEADS, HEAD_SIZE), 0.1),
        "rw_lnw": 1.0 + nrm((N_RWKV, D), 0.02),
        "rw_lnb": nrm((N_RWKV, D), 0.02),
        "sc_win": nrm((N_CONV, D, 3 * D), inv),
        "sc_conv": nrm((N_CONV, CONV_WIDTH, D), CONV_WIDTH ** -0.5),
        "sc_wout": nrm((N_CONV, D, D), inv),
        "ffn_w13": nrm((DEPTH, D, 2 * F), inv),
        "ffn_w2": nrm((DEPTH, F, D), F ** -0.5),
        "final_g": 1.0 + nrm((D,), 0.02),
    }


def reference(x, c, ctx, c_ctx, norm1_g, norm2_g, ada_w, ada_b,
              rw_mix, rw_wr, rw_wk, rw_wv, rw_wo, rw_w0, rw_w1, rw_w2,
              rw_a0, rw_a1, rw_a2, rw_g1, rw_g2, rw_kk, rw_ka, rw_rk, rw_lnw, rw_lnb,
              sc_win, sc_conv, sc_wout, ffn_w13, ffn_w2, final_g):
    cond_x = jax.nn.silu(c)
    cond_c = jax.nn.silu(c_ctx)
    for i in range(DEPTH):
        last = i == DEPTH - 1
        is_rwkv = i % N_MIXERS == 0
        j = i // N_MIXERS
        mod_x = (cond_x @ ada_w[i] + ada_b[i])[:, None, :]
        sh1_x, sc1_x, gt1_x, sh2_x, sc2_x, gt2_x = jnp.split(mod_x, 6, axis=-1)
        hx = modulate(rmsnorm(x, norm1_g[i]), sh1_x, sc1_x)
        ctx_used = is_rwkv or not last
        if ctx_used:
            mod_c = (cond_c @ ada_w[i] + ada_b[i])[None, None, :]
            sh1_c, sc1_c, gt1_c, sh2_c, sc2_c, gt2_c = jnp.split(mod_c, 6, axis=-1)
            hc = modulate(rmsnorm(ctx, norm1_g[i]), sh1_c, sc1_c)
        if is_rwkv:
            yx, yc = rwkv7_mix(hx, hc, not last, rw_mix[j], rw_wr[j], rw_wk[j], rw_wv[j], rw_wo[j],
                               rw_w0[j], rw_w1[j], rw_w2[j], rw_a0[j], rw_a1[j], rw_a2[j],
                               rw_g1[j], rw_g2[j], rw_kk[j], rw_ka[j], rw_rk[j], rw_lnw[j], rw_lnb[j])
        else:
            yx = short_conv(hx, sc_win[j], sc_conv[j], sc_wout[j])
            yc = short_conv(hc, sc_win[j], sc_conv[j], sc_wout[j]) if not last else None
        x = x + gt1_x * yx
        x = x + gt2_x * swiglu(modulate(rmsnorm(x, norm2_g[i]), sh2_x, sc2_x), ffn_w13[i], ffn_w2[i])
        if not last:
            ctx = ctx + gt1_c * yc
            ctx = ctx + gt2_c * swiglu(modulate(rmsnorm(ctx, norm2_g[i]), sh2_c, sc2_c), ffn_w13[i], ffn_w2[i])
    return rmsnorm(x, final_g)
```

```python
import contextlib
import os
import numpy as np
import concourse.bass as bass
import concourse.mybir as mybir
from concourse.bass_utils import run_bass_kernel_spmd

F32 = mybir.dt.float32
BF16 = mybir.dt.bfloat16
F32R = mybir.dt.float32r
AF = mybir.ActivationFunctionType
ALU = mybir.AluOpType
AX = mybir.AxisListType

D = 2048
T = 2048
TC = 256
FF = 5632
KC = 16
NH = 32
CH = 64
NFC = FF // 128
TA = TC + T
NCHK = TA // CH
LWS = 0.6065306597126334
C_MASKF, C_MASKB, C_ID, C_MTF, C_MTB, C_HSEL, C_BONES, C_RESET, C_ID128 = 0, 128, 256, 320, 384, 448, 450, 578, 834
NCST = 834 + 128
EPS = 1e-6
GN_EPS = 64e-5


class Op:
    __slots__ = ("eng", "fn", "deps", "seq", "is_dma", "sem", "semval", "sig", "idx")


class Prog:
    ENGS = ("pe", "act", "dve", "pool", "sp")

    def __init__(self):
        self.ops = []
        self.last_w = {}
        self.readers = {}
        self.dma_cnt = {}
        self.pending = {}
        self.last_eng = {}
        self.last_dma = {}

    def barrier(self):
        snap = set(self.last_eng.values()) | set(self.last_dma.values())
        for e in self.ENGS:
            self.pending[e] = set(snap) | self.pending.get(e, set())

    def add(self, eng, fn, reads=(), writes=(), dma=None):
        op = Op()
        op.eng = eng
        op.fn = fn
        op.is_dma = dma is not None
        op.sig = False
        op.seq = 0
        op.idx = len(self.ops)
        deps = set()
        for r in reads:
            w = self.last_w.get(r)
            if w is not None:
                deps.add(w)
        for k in writes:
            w = self.last_w.get(k)
            if w is not None:
                deps.add(w)
            rl = self.readers.get(k)
            if rl:
                deps.update(rl)
        pend = self.pending.pop(eng, None)
        if pend:
            deps.update(pend)
        if eng == "pe" and not op.is_dma:
            deps = {d for d in deps if d.is_dma or d.eng != "pe"}
        op.deps = deps
        if op.is_dma:
            self.last_dma[dma] = op
        else:
            self.last_eng[eng] = op
        for d in deps:
            d.sig = True
        for r in reads:
            self.readers.setdefault(r, []).append(op)
        for k in writes:
            self.last_w[k] = op
            self.readers[k] = []
        if dma is not None:
            c = self.dma_cnt.get(dma, 0) + 16
            self.dma_cnt[dma] = c
            op.sem = dma
            op.semval = c
        self.ops.append(op)
        return op

    def flush(self, nc, final=False):
        if not hasattr(self, "emitted"):
            self.emitted = 0
            self.cnt = {e: 0 for e in self.ENGS}
            self.eng_sem = {e: nc.alloc_semaphore(name="s_" + e) for e in self.ENGS}
            self.dma_sem = {}
            self.waited = {e: {} for e in self.ENGS}
        ops = self.ops[self.emitted:]
        self.emitted = len(self.ops)
        lasts = set(self.last_eng.values()) | set(self.last_dma.values())
        for w in lasts:
            w.sig = True
        for op in ops:
            if op.is_dma:
                if op.sem not in self.dma_sem:
                    self.dma_sem[op.sem] = nc.alloc_semaphore(name="d_%d" % len(self.dma_sem))
                continue
            if op.sig:
                self.cnt[op.eng] += 1
                op.seq = self.cnt[op.eng]
        per_eng = {e: [] for e in self.ENGS}
        for op in ops:
            per_eng[op.eng].append(op)
        eng_sem, dma_sem = self.eng_sem, self.dma_sem

        def run(e, ename):
            waited = self.waited[ename]

            def do_waits(deps):
                need = {}
                for d in deps:
                    if d.is_dma:
                        key = ("d", d.sem)
                        sem = dma_sem[d.sem]
                        val = d.semval
                    else:
                        key = ("e", d.eng)
                        sem = eng_sem[d.eng]
                        val = d.seq
                    if need.get(key, (None, 0))[1] < val:
                        need[key] = (sem, val)
                for key, (sem, val) in need.items():
                    if waited.get(key, 0) >= val:
                        continue
                    e.wait_ge(sem, val)
                    waited[key] = val

            for op in per_eng[ename]:
                do_waits(op.deps)
                inst = op.fn(e)
                if op.is_dma:
                    inst.then_inc(dma_sem[op.sem], 16)
                elif op.sig:
                    inst.then_inc(eng_sem[ename], 1)
            if final and ename == "sp":
                do_waits(lasts)

        with nc.Block() as block:
            @block.tensor
            def _(e):
                run(e, "pe")

            @block.scalar
            def _(e):
                run(e, "act")

            @block.vector
            def _(e):
                run(e, "dve")

            @block.gpsimd
            def _(e):
                run(e, "pool")

            @block.sync
            def _(e):
                run(e, "sp")
        self.barrier()


def _fm(v):
    return np.ascontiguousarray(np.asarray(v, np.float32).reshape(-1, 128).T)


PV_NAMES = (["n1g0", "n1g1", "n2g0", "n2g1", "fg"] + ["mix%d" % m for m in range(6)]
            + ["w00", "w01", "a00", "a01", "kk", "ka", "rk", "cw0", "cw1", "cw2"])
PV_OFF = {n: i * 16 for i, n in enumerate(PV_NAMES)}
PV_OFF["adab0"] = len(PV_NAMES) * 16
PV_OFF["adab1"] = PV_OFF["adab0"] + 96
NPV = PV_OFF["adab1"] + 96


def make_cst():
    c = np.zeros((128, NCST), np.float32)
    j = (np.arange(128) % 64)[:, None]
    i = np.arange(64)[None, :]
    c[:, C_MASKF:C_MASKF + 64] = (j < i)
    c[:, C_MASKF + 64:C_MASKF + 128] = (j <= i)
    c[:, C_MASKB:C_MASKB + 64] = (j > i)
    c[:, C_MASKB + 64:C_MASKB + 128] = (j >= i)
    c[:, C_ID:C_ID + 64] = (j == i)
    c[:, C_MTF:C_MTF + 64] = (i < j)
    c[:, C_MTB:C_MTB + 64] = (i > j)
    c[0:64, C_HSEL] = 1.0
    c[64:128, C_HSEL + 1] = 1.0
    p = np.arange(128)[:, None] // 64
    q = np.arange(128)[None, :] // 64
    c[:, C_BONES:C_BONES + 128] = (p == q)
    c[:, C_RESET:C_RESET + 256] = 1.0
    c[:, C_RESET:C_RESET + 256:64] = 0.0
    c[:, C_ID128:C_ID128 + 128] = np.eye(128, dtype=np.float32)
    return c


def pack_pvec(inp):
    pv = np.zeros((128, NPV), np.float32)

    def put(name, v):
        a = _fm(v)
        pv[:, PV_OFF[name]:PV_OFF[name] + a.shape[1]] = a

    put("n1g0", inp["norm1_g"][0]); put("n1g1", inp["norm1_g"][1])
    put("n2g0", inp["norm2_g"][0]); put("n2g1", inp["norm2_g"][1])
    put("fg", inp["final_g"])
    for m in range(6):
        put("mix%d" % m, inp["rw_mix"][0, m])
    put("w00", inp["rw_w0"][0, 0]); put("w01", inp["rw_w0"][0, 1])
    put("a00", inp["rw_a0"][0, 0]); put("a01", inp["rw_a0"][0, 1])
    put("kk", inp["rw_kk"][0]); put("ka", inp["rw_ka"][0]); put("rk", inp["rw_rk"][0].reshape(-1))
    for i in range(3):
        put("cw%d" % i, inp["sc_conv"][0, i])
    put("adab0", inp["ada_b"][0]); put("adab1", inp["ada_b"][1])
    return pv


class Builder:
    def __init__(self, start_layer=0, dbg=False, stop_after=None, only_hq=None, scan_limit=None):
        self.dbg = dbg
        self.only_hq = only_hq
        self.proj01 = bool(scan_limit) and scan_limit <= 8 and only_hq is not None and dbg == 'sim'
        self.scan_limit = scan_limit
        self.dumps = []
        nc = self.nc = bass.Bass("TRN2", target_bir_lowering=False)
        P = self.P = Prog()
        di = lambda name, shape: nc.dram_tensor(name, shape, F32, kind="ExternalInput").ap()
        self.xT = di("xT", [D, T])
        self.ctxT = di("ctxT", [D, TC])
        self.cond = di("cond", [128, KC, 2])
        self.pvec = di("pvec", [128, NPV])
        self.ada_w = di("ada_w", [2, D, 6 * D])
        self.sc_win = di("sc_win", [D, 3 * D])
        self.sc_wout = di("sc_wout", [D, D])
        self.ffn_w13 = di("ffn_w13", [2, D, 2 * FF])
        self.ffn_w2 = di("ffn_w2", [2, FF, D])
        self.rw_wr = di("rw_wr", [D, D]); self.rw_wk = di("rw_wk", [D, D])
        self.rw_wv = di("rw_wv", [D, D]); self.rw_wo = di("rw_wo", [D, D])
        self.rw_w1 = di("rw_w1", [2, D, 96]); self.rw_w2 = di("rw_w2", [2, 96, D])
        self.rw_a1 = di("rw_a1", [2, D, 96]); self.rw_a2 = di("rw_a2", [2, 96, D])
        self.rw_g1 = di("rw_g1", [D, 256]); self.rw_g2 = di("rw_g2", [256, D])
        self.lnwb = di("lnwb", [2, 128, D])
        self.cst = di("cst", [128, NCST])
        self.outT = nc.dram_tensor("outT", [D, T], F32, kind="ExternalOutput").ap()
        dk = "ExternalOutput" if dbg else "Internal"
        dr = lambda name, shape, dt=F32: nc.dram_tensor(name, shape, dt, kind=dk).ap()
        self.HXD = dr("HXD", [D, TA], BF16)
        self.LBKD = [dr("LBKD%d" % d, [D, NCHK, 128]) for d in range(2)]
        self.RARD = [dr("RARD%d" % d, [D, NCHK, 128]) for d in range(2)]
        self.DCD = [dr("DCD%d" % d, [D, NCHK]) for d in range(2)]
        self.VD = dr("VD", [TA, D])
        self.GD = dr("GD", [T, D])
        self.BOND = dr("BOND", [T, NH])
        self.YD = [dr("YD%d" % d, [T, D]) for d in range(2)]
        kind = "ExternalOutput" if dbg else "Internal"
        self.XT = nc.dram_tensor("XT", [D, T], F32, kind=kind).ap()
        self.ZD = nc.dram_tensor("ZD", [D, T + 2], F32, kind="Internal").ap()
        self.GBD = nc.dram_tensor("GBD", [D, T], BF16, kind="Internal").ap()
        sb = lambda name, shape, dt: nc.alloc_sbuf_tensor(name, shape, dt).ap()
        self.SQ = [sb("sq%d" % i, [128, 512], F32R) for i in range(2)]
        self.PV = sb("pv", [128, NPV], F32)
        self.CT = sb("ct", [128, KC, 2], F32)
        self.MOD = [sb("mod%d" % i, [128, 96, 2], F32) for i in range(2)]
        self.MS = [sb("ms%d" % i, [128, 6, 16], F32) for i in range(2)]
        self.MSC = sb("msc", [128, 2, 16], F32)
        self.ONES = sb("ones", [128, 128], F32R)
        self.ONESF = sb("onesf", [128, 128], F32)
        self.ZERO = sb("zero", [128, 512], F32)
        self.CST = sb("cstt", [128, NCST], F32)
        self.OMKA = sb("omka", [128, 16], F32)
        self.BONESr = sb("bonesr", [128, 128], F32R)
        self.HSELr = sb("hselr", [128, 2], F32R)
        self.PRr = sb("prr", [128, 256], F32R)
        self.IDr = sb("idr", [128, 128], F32R)
        self.IDb = sb("idb", [128, 128], BF16)
        self.ps = [nc.alloc_psum_tensor("ps%d" % i, [128, 512], F32).ap() for i in range(8)]
        self.ws_i = 0
        self.ps_i = 0
        self.sq_i = 0
        self.tmp_i = 0
        self.stop_after = stop_after
        fin = lambda: P.flush(nc, final=True)
        with self.std_bufs("a"):
            self.prologue()
            self.adaln()
            if dbg:
                self.dump("d_ms0", self.MS[0], [128, 6, 16], "ms0")
                self.dump("d_ms1", self.MS[1], [128, 6, 16], "ms1")
                self.dump("d_msc", self.MSC, [128, 2, 16], "msc")
            if start_layer == 0:
                self.rwkv_norm()
                P.barrier()
                self.rwkv_proj(only_tiles=([int(x) for x in os.environ['PROJ_TILES'].split(',')] if os.environ.get('PROJ_TILES') else ([0, 1] if (stop_after == "proj01" or getattr(self, 'proj01', False)) else None)))
                if stop_after in ("proj", "proj01"):
                    return fin()
                P.flush(nc)
        if start_layer == 0:
            with self.scan_bufs():
                self.rwkv_scan()
                if stop_after == "scan":
                    return fin()
                P.flush(nc)
        with self.std_bufs("b"):
            if start_layer == 0:
                self.rwkv_readout()
                if stop_after == "mix0":
                    return fin()
                P.barrier()
                self.ffn(0)
                if stop_after == "l0":
                    return fin()
                P.barrier()
            else:
                self.copy_x_to_XT()
            self.layer1_mixer()
            P.barrier()
            self.ffn(1, final=True)
            fin()

    @contextlib.contextmanager
    def std_bufs(self, sfx):
        nc = self.nc
        with contextlib.ExitStack() as st:
            sb = lambda name, shape, dt: st.enter_context(nc.sbuf_tensor(name + sfx, shape, dt)).ap()
            self.WS = [sb("ws%d" % i, [128, 8192], BF16) for i in range(4)]
            self.BIGA = sb("biga", [128, 24704], BF16)
            self.XTB = sb("xtb", [128, KC, 512], F32)
            self.HB = sb("hb", [128, KC, 512], BF16)
            self.TMP = [sb("tmp%d" % i, [128, 512], F32) for i in range(3)]
            self.RS = sb("rs", [128, 512], F32)
            self.RWB = sb("rwb", [128, 10240], BF16)
            yield

    def dump(self, name, ap, shape, key, dt=F32):
        d = self.nc.dram_tensor(name, shape, dt, kind="ExternalOutput").ap()
        self.P.add("sp", lambda e: e.dma_start(out=d, in_=ap), reads=[key], writes=[name], dma=name)
        self.dumps.append(name)

    def pvc(self, name, c=None):
        o = PV_OFF[name]
        if c is None:
            return self.PV[:, o:o + 16]
        return self.PV[:, o + c:o + c + 1]

    def next_ws(self):
        i = self.ws_i
        self.ws_i = (i + 1) % 4
        return i

    def next_ps(self, lo=0, hi=4):
        i = lo + self.ps_i % (hi - lo)
        self.ps_i += 1
        return i

    def load_w(self, src3, kp, nk, ncols):
        s = self.next_ws()
        view = self.WS[s][:, 0:nk * ncols].rearrange("p (k n) -> p k n", n=ncols)
        dst = view[0:kp, :, :]
        self.P.add("pool", lambda e: e.dma_start(out=dst, in_=src3), writes=["ws%d" % s], dma="ws%d" % s)
        return s, view

    def prologue(self):
        P = self.P
        P.add("sp", lambda e: e.dma_start(out=self.PV, in_=self.pvec), writes=["pv"], dma="pv")
        P.add("sp", lambda e: e.dma_start(out=self.CT, in_=self.cond), writes=["ct"], dma="ct")
        P.add("sp", lambda e: e.dma_start(out=self.CST, in_=self.cst), writes=["cst"], dma="cst")
        P.add("dve", lambda e: e.tensor_scalar(out=self.OMKA, in0=self.pvc("ka"), scalar1=-1.0, scalar2=1.0,
                                               op0=ALU.mult, op1=ALU.add), reads=["pv"], writes=["omka"])
        P.add("dve", lambda e: e.memset(self.ONESF, 1.0), writes=["onesf"])
        P.add("dve", lambda e: e.tensor_copy(out=self.ONES, in_=self.ONESF), reads=["onesf"], writes=["ones"])
        P.add("dve", lambda e: e.memset(self.ZERO, 0.0), writes=["zero"])
        P.add("dve", lambda e: e.tensor_copy(out=self.IDr, in_=self.CST[:, C_ID128:C_ID128 + 128]), reads=["cst"], writes=["idr"])
        P.add("dve", lambda e: e.tensor_copy(out=self.IDb, in_=self.CST[:, C_ID128:C_ID128 + 128]), reads=["cst"], writes=["idb"])
        P.add("act", lambda e: e.activation(out=self.CT, in_=self.CT, func=AF.Silu), reads=["ct"], writes=["ct"])

    def adaln(self):
        P = self.P
        for i in range(2):
            psA = self.ps[6 + i]
            wv = self.ada_w[i].rearrange("(kc p) n -> p kc n", p=128)
            for nb in range(48):
                s = self.next_ws()
                view = self.WS[s].bitcast(F32).rearrange("p (k n) -> p k n", n=256)
                src = wv[:, :, nb * 256:(nb + 1) * 256]
                P.add("sp", lambda e, view=view, src=src: e.dma_start(out=view, in_=src),
                      writes=["ws%d" % s], dma="wsa%d" % s)
                for jj in range(2):
                    j = nb * 2 + jj
                    for kc in range(KC):
                        P.add("pe", lambda e, view=view, jj=jj, j=j, kc=kc, psA=psA: e.matmul(
                            psA[:, 2 * j:2 * j + 2], lhsT=view[:, kc, jj * 128:(jj + 1) * 128], rhs=self.CT[:, kc, :],
                            start=(kc == 0), stop=(kc == KC - 1)),
                            reads=["ws%d" % s, "ct"], writes=["ps%d" % (6 + i)])
            ab = self.PV[:, PV_OFF["adab%d" % i]:PV_OFF["adab%d" % i] + 96]
            P.add("dve", lambda e, i=i, psA=psA, ab=ab: e.tensor_tensor(
                out=self.MOD[i], in0=psA[:, 0:192].rearrange("p (j t) -> p j t", t=2),
                in1=ab.unsqueeze(2).to_broadcast([128, 96, 2]), op=ALU.add),
                reads=["ps%d" % (6 + i), "pv"], writes=["mod%d" % i])
            M = self.MOD[i]
            MS = self.MS[i]
            n1g = self.pvc("n1g%d" % i)
            n2g = self.pvc("n2g%d" % i)
            P.add("dve", lambda e, M=M, MS=MS, n1g=n1g: e.scalar_tensor_tensor(
                out=MS[:, 0, :], in0=M[:, 16:32, 0], scalar=1.0, in1=n1g, op0=ALU.add, op1=ALU.mult),
                reads=["mod%d" % i, "pv"], writes=["ms%d" % i])
            P.add("dve", lambda e, M=M, MS=MS: e.tensor_copy(out=MS[:, 1, :], in_=M[:, 0:16, 0]),
                  reads=["mod%d" % i], writes=["ms%d" % i])
            P.add("dve", lambda e, M=M, MS=MS: e.tensor_copy(out=MS[:, 2, :], in_=M[:, 32:48, 0]),
                  reads=["mod%d" % i], writes=["ms%d" % i])
            P.add("dve", lambda e, M=M, MS=MS, n2g=n2g: e.scalar_tensor_tensor(
                out=MS[:, 3, :], in0=M[:, 64:80, 0], scalar=1.0, in1=n2g, op0=ALU.add, op1=ALU.mult),
                reads=["mod%d" % i, "pv"], writes=["ms%d" % i])
            P.add("dve", lambda e, M=M, MS=MS: e.tensor_copy(out=MS[:, 4, :], in_=M[:, 48:64, 0]),
                  reads=["mod%d" % i], writes=["ms%d" % i])
            P.add("dve", lambda e, M=M, MS=MS: e.tensor_copy(out=MS[:, 5, :], in_=M[:, 80:96, 0]),
                  reads=["mod%d" % i], writes=["ms%d" % i])
            if i == 0:
                P.add("dve", lambda e, M=M, n1g=n1g: e.scalar_tensor_tensor(
                    out=self.MSC[:, 0, :], in0=M[:, 16:32, 1], scalar=1.0, in1=n1g, op0=ALU.add, op1=ALU.mult),
                    reads=["mod0", "pv"], writes=["msc"])
                P.add("dve", lambda e, M=M: e.tensor_copy(out=self.MSC[:, 1, :], in_=M[:, 0:16, 1]),
                      reads=["mod0"], writes=["msc"])

    def norm_mod(self, X, xkey, A, B, akey, out, okey, n, bkey=None):
        P = self.P
        psn = 6
        psN = self.ps[psn]
        for c in range(KC):
            q = self.sq_i % 2
            self.sq_i += 1
            SQ = self.SQ[q]
            P.add("act", lambda e, SQ=SQ, c=c: e.activation(out=SQ[:, :n], in_=X[:, c, :n], func=AF.Square),
                  reads=[xkey], writes=["sq%d" % q])
            P.add("pe", lambda e, SQ=SQ, c=c: e.matmul(psN[:, :n], lhsT=self.ONES, rhs=SQ[:, :n],
                                                      start=(c == 0), stop=(c == KC - 1)),
                  reads=["sq%d" % q, "ones"], writes=["ps%d" % psn])
        P.add("act", lambda e: e.activation(out=self.RS[:, :n], in_=psN[:, :n], func=AF.Sqrt, scale=1.0 / D, bias=EPS),
              reads=["ps%d" % psn], writes=["rs"])
        P.add("dve", lambda e: e.reciprocal(out=self.RS[:, :n], in_=self.RS[:, :n]), reads=["rs"], writes=["rs"])
        for c in range(KC):
            q = self.tmp_i % 3
            self.tmp_i += 1
            TM = self.TMP[q]
            P.add("dve", lambda e, TM=TM, c=c: e.scalar_tensor_tensor(
                out=TM[:, :n], in0=X[:, c, :n], scalar=A[:, c:c + 1], in1=self.RS[:, :n], op0=ALU.mult, op1=ALU.mult),
                reads=[xkey, "rs", akey], writes=["tmp%d" % q])
            if B is not None:
                P.add("act", lambda e, TM=TM, c=c: e.activation(out=out[:, c, :n], in_=TM[:, :n], func=AF.Identity,
                                                               bias=B[:, c:c + 1]),
                      reads=["tmp%d" % q, akey], writes=[okey])
            else:
                P.add("act", lambda e, TM=TM, c=c: e.copy(out=out[:, c, :n], in_=TM[:, :n]),
                      reads=["tmp%d" % q], writes=[okey])

    def copy_x_to_XT(self):
        P = self.P
        for t in range(T // 512):
            src = self.xT.rearrange("(c p) t -> p c t", p=128)[:, :, t * 512:(t + 1) * 512]
            dst = self.XT.rearrange("(c p) t -> p c t", p=128)[:, :, t * 512:(t + 1) * 512]
            P.add("sp", lambda e, src=src: e.dma_start(out=self.XTB, in_=src), writes=["xtb"], dma="xtb")
            P.add("sp", lambda e, dst=dst: e.dma_start(out=dst, in_=self.XTB), reads=["xtb"], writes=["XT%d" % t], dma="xtbo")

    def rwkv_norm(self):
        P = self.P
        HXv = self.HXD.rearrange("(c p) t -> p c t", p=128)
        tiles = [(self.ctxT, 0, TC, 0, self.MSC[:, 0, :], self.MSC[:, 1, :], "msc")]
        for t in range(T // 512):
            tiles.append((self.xT, t * 512, 512, TC + t * 512, self.MS[0][:, 0, :], self.MS[0][:, 1, :], "ms0"))
        for (src, s0, n, a0, A, B, akey) in tiles:
            sv = src.rearrange("(c p) t -> p c t", p=128)[:, :, s0:s0 + n]
            P.add("sp", lambda e, sv=sv, n=n: e.dma_start(out=self.XTB[:, :, :n], in_=sv), writes=["xtb"], dma="xtb")
            self.norm_mod(self.XTB, "xtb", A, B, akey, self.HB, "hb", n)
            P.add("sp", lambda e, a0=a0, n=n: e.dma_start(out=HXv[:, :, a0:a0 + n], in_=self.HB[:, :, :n]),
                  reads=["hb"], writes=[], dma="hxst")

    def rwkv_proj(self, only_tiles=None):
        P = self.P
        n = 256
        nch = n // CH
        HXv = self.HXD.rearrange("(c p) t -> p c t", p=128)
        HBa = self.HB[:, :, 0:n]
        HBb = self.HB[:, :, n:2 * n]
        XM = self.BIGA[:, 0:4096].rearrange("p (c t) -> p c t", t=n)
        XR = self.BIGA[:, 4096:8192].rearrange("p (c t) -> p c t", t=n)
        XK = self.BIGA[:, 8192:12288].rearrange("p (c t) -> p c t", t=n)
        VT = self.BIGA[:, 16384:24576].bitcast(F32).rearrange("p (s f) -> p s f", f=D)
        L2W = self.RWB[:, 0:8192].rearrange("p (w f) -> p w f", f=D)
        HL = self.RWB[:, 8192:9216].rearrange("p (w t) -> p w t", t=n)
        HG = self.RWB[:, 9216:9728].rearrange("p (w t) -> p w t", t=n)
        BONS = self.RWB[:, 9728:9856].bitcast(F32).rearrange("p (s h) -> p s h", h=NH)
        XQ = self.XTB.rearrange("p c t -> p (c t)")

        def xq(k, w=1):
            return XQ[:, k * n:(k + w) * n]

        Qr, Qk, Qkk, Qsq, Qsig, Qa, Qcs, Qcx, Qer, Qek, Qea, Qt10, Qt11, Qkd, Qt2, Qpr, Qinv, Qtot = range(18)
        QL = [18, 22]
        QR = [20, 24]
        QDC = 26
        cst = self.CST
        MRESET = cst[:, C_RESET:C_RESET + n]
        BONESr = self.BONESr
        HSELr = self.HSELr
        P.add("dve", lambda e: e.tensor_copy(out=BONESr, in_=cst[:, C_BONES:C_BONES + 128]), reads=["cst"], writes=["bones"])
        P.add("dve", lambda e: e.tensor_copy(out=HSELr, in_=cst[:, C_HSEL:C_HSEL + 2]), reads=["cst"], writes=["hsel"])
        for w, src in enumerate([self.rw_w2[0], self.rw_w2[1], self.rw_a2[0], self.rw_a2[1]]):
            P.add("pool", lambda e, w=w, src=src: e.dma_start(out=L2W[0:96, w, :], in_=src), writes=["l2w"], dma="l2w%d" % w)
        wr = self.rw_wr.rearrange("(kc p) n -> p kc n", p=128)
        wk = self.rw_wk.rearrange("(kc p) n -> p kc n", p=128)
        wv = self.rw_wv.rearrange("(kc p) n -> p kc n", p=128)
        g1 = self.rw_g1.rearrange("(kc p) n -> p kc n", p=128)
        g2 = self.rw_g2.rearrange("(kc p) n -> p kc n", p=128)
        w1 = [self.rw_w1[d].rearrange("(kc p) n -> p kc n", p=128) for d in range(2)]
        a1 = [self.rw_a1[d].rearrange("(kc p) n -> p kc n", p=128) for d in range(2)]
        psB = self.ps[7]

        def mix(m, dst, dkey):
            for c in range(KC):
                P.add("dve", lambda e, c=c: e.scalar_tensor_tensor(
                    out=dst[:, c, :], in0=HBb[:, c, :], scalar=self.pvc("mix%d" % m, c), in1=HBa[:, c, :],
                    op0=ALU.mult, op1=ALU.add), reads=["hb", "pv"], writes=[dkey])

        def mm_fm(view, s, col0, M, rhs3, rkey, pi, kp=128, nk=KC):
            for kc in range(nk):
                P.add("pe", lambda e, kc=kc: e.matmul(self.ps[pi][0:M, :n], lhsT=view[0:kp, kc, col0:col0 + M],
                                                     rhs=rhs3[0:kp, kc, :], start=(kc == 0), stop=(kc == nk - 1)),
                      reads=["ws%d" % s, rkey], writes=["ps%d" % pi])

        for ti in range(1 + T // n):
            if only_tiles is not None and ti not in only_tiles:
                continue
            is_ctx = ti == 0
            a0 = 0 if is_ctx else TC + (ti - 1) * n
            t0 = 0 if is_ctx else (ti - 1) * n
            ch0 = a0 // CH
            P.add("sp", lambda e, a0=a0: e.dma_start(out=HBa, in_=HXv[:, :, a0:a0 + n]), writes=["hb"], dma="hxld")
            HB4 = HBb.rearrange("p c (r w) -> p c r w", w=CH)
            if is_ctx:
                P.add("sp", lambda e: e.dma_start(out=HBb[:, 0:8, 1:n], in_=HXv[:, 0:8, 0:n - 1]), writes=["hb"], dma="shld0")
                P.add("sp", lambda e: e.dma_start(out=HBb[:, 8:16, 0:n - 1], in_=HXv[:, 8:16, 1:n]), writes=["hb"], dma="shld1")
                P.add("dve", lambda e: e.memset(HBb[:, 0:8, 0:1], 0.0), writes=["hb"])
                P.add("dve", lambda e: e.memset(HBb[:, 8:16, n - 1:n], 0.0), writes=["hb"])
            else:
                P.add("sp", lambda e, a0=a0: e.dma_start(out=HBb[:, 0:4, 1:n], in_=HXv[:, 0:4, a0:a0 + n - 1]), writes=["hb"], dma="shld0")
                P.add("sp", lambda e, a0=a0: e.dma_start(out=HBb[:, 4:8, 0:n - 1], in_=HXv[:, 4:8, a0 + 1:a0 + n]), writes=["hb"], dma="shld1")
                if t0 == 0:
                    P.add("sp", lambda e, a0=a0: e.dma_start(out=HBb[:, 8:12, CH:n], in_=HXv[:, 8:12, a0:a0 + n - CH]), writes=["hb"], dma="shld2")
                    P.add("dve", lambda e: e.memset(HBb[:, 8:12, 0:CH], 0.0), writes=["hb"])
                else:
                    P.add("sp", lambda e, a0=a0: e.dma_start(out=HBb[:, 8:12, :], in_=HXv[:, 8:12, a0 - CH:a0 + n - CH]), writes=["hb"], dma="shld2")
                if t0 == T - n:
                    P.add("sp", lambda e, a0=a0: e.dma_start(out=HBb[:, 12:16, 0:n - CH], in_=HXv[:, 12:16, a0 + CH:a0 + n]), writes=["hb"], dma="shld3")
                    P.add("dve", lambda e: e.memset(HBb[:, 12:16, n - CH:n], 0.0), writes=["hb"])
                else:
                    P.add("sp", lambda e, a0=a0: e.dma_start(out=HBb[:, 12:16, :], in_=HXv[:, 12:16, a0 + CH:a0 + n + CH]), writes=["hb"], dma="shld3")
                P.add("dve", lambda e: e.memset(HB4[:, 0:4, :, 0:1], 0.0), writes=["hb"])
                P.add("dve", lambda e: e.memset(HB4[:, 4:8, :, CH - 1:CH], 0.0), writes=["hb"])
            P.add("dve", lambda e: e.tensor_tensor(out=HBb, in0=HBb, in1=HBa, op=ALU.subtract), reads=["hb"], writes=["hb"])
            for m, wsrc, w_off, func in ((1, w1, 0, AF.Tanh), (4, a1, 2, AF.Identity)):
                mix(m, XM, "bg0")
                for d in range(2):
                    s, view = self.load_w(wsrc[d], 128, KC, 96)
                    pi = self.next_ps(0, 6)
                    mm_fm(view, s, 0, 96, XM, "bg0", pi)
                    P.add("act", lambda e, pi=pi, d=d, w_off=w_off, func=func: e.activation(
                        out=HL[0:96, w_off + d, :], in_=self.ps[pi][0:96, :n], func=func),
                        reads=["ps%d" % pi], writes=["hl"])
            if not is_ctx:
                mix(5, XM, "bg0")
                s, view = self.load_w(g1, 128, KC, 256)
                for cc in range(2):
                    pi = self.next_ps(0, 6)
                    mm_fm(view, s, cc * 128, 128, XM, "bg0", pi)
                    P.add("act", lambda e, pi=pi, cc=cc: e.activation(out=HG[:, cc, :], in_=self.ps[pi][:, :n], func=AF.Sigmoid),
                          reads=["ps%d" % pi], writes=["hg"])
                for nb in range(4):
                    s, view = self.load_w(g2[:, :, nb * 512:(nb + 1) * 512], 128, 2, 512)
                    for su in range(n // 128):
                        pi = self.next_ps(0, 6)
                        for kc in range(2):
                            P.add("pe", lambda e, kc=kc, su=su, view=view, pi=pi: e.matmul(
                                self.ps[pi], lhsT=HG[:, kc, su * 128:(su + 1) * 128], rhs=view[:, kc, :],
                                start=(kc == 0), stop=(kc == 1)), reads=["ws%d" % s, "hg"], writes=["ps%d" % pi])
                        P.add("act", lambda e, pi=pi, su=su, nb=nb: e.copy(out=VT[:, su, nb * 512:(nb + 1) * 512], in_=self.ps[pi]),
                              reads=["ps%d" % pi], writes=["bg1"])
                P.add("sp", lambda e, t0=t0: e.dma_start(
                    out=self.GD[t0:t0 + n, :].rearrange("(s p) f -> p s f", p=128), in_=VT),
                    reads=["bg1"], writes=[], dma="gst")
            mix(3, XM, "bg0")
            for nb in range(4):
                s, view = self.load_w(wv[:, :, nb * 512:(nb + 1) * 512], 128, KC, 512)
                for su in range(n // 128):
                    pi = self.next_ps(0, 6)
                    for kc in range(KC):
                        P.add("pe", lambda e, kc=kc, su=su, view=view, pi=pi: e.matmul(
                            self.ps[pi], lhsT=XM[:, kc, su * 128:(su + 1) * 128], rhs=view[:, kc, :],
                            start=(kc == 0), stop=(kc == KC - 1)), reads=["ws%d" % s, "bg0"], writes=["ps%d" % pi])
                    P.add("act", lambda e, pi=pi, su=su, nb=nb: e.copy(out=VT[:, su, nb * 512:(nb + 1) * 512], in_=self.ps[pi]),
                          reads=["ps%d" % pi], writes=["bg1"])
            P.add("sp", lambda e, a0=a0: e.dma_start(
                out=self.VD[a0:a0 + n, :].rearrange("(s p) f -> p s f", p=128), in_=VT),
                reads=["bg1"], writes=[], dma="vst")
            mix(0, XR, "bgr")
            mix(2, XK, "bgk")
            for j in range(KC):
                sr, vr = self.load_w(wr[:, :, j * 128:(j + 1) * 128], 128, KC, 128)
                sk, vk = self.load_w(wk[:, :, j * 128:(j + 1) * 128], 128, KC, 128)
                p_r = self.next_ps(0, 6)
                mm_fm(vr, sr, 0, 128, XR, "bgr", p_r)
                p_k = self.next_ps(0, 6)
                mm_fm(vk, sk, 0, 128, XK, "bgk", p_k)
                P.add("act", lambda e, p_r=p_r: e.copy(out=xq(Qr), in_=self.ps[p_r][:, :n]), reads=["ps%d" % p_r], writes=["xq%d" % Qr])
                P.add("act", lambda e, p_k=p_k: e.copy(out=xq(Qk), in_=self.ps[p_k][:, :n]), reads=["ps%d" % p_k], writes=["xq%d" % Qk])
                P.add("dve", lambda e, j=j: e.tensor_scalar(out=xq(Qkk), in0=xq(Qk), scalar1=self.pvc("kk", j), scalar2=None, op0=ALU.mult),
                      reads=["xq%d" % Qk, "pv"], writes=["xq%d" % Qkk])
                q = self.sq_i % 2
                self.sq_i += 1
                SQ = self.SQ[q]
                P.add("act", lambda e, SQ=SQ: e.activation(out=SQ[:, :n], in_=xq(Qkk), func=AF.Square), reads=["xq%d" % Qkk], writes=["sq%d" % q])
                P.add("pe", lambda e, SQ=SQ: e.matmul(self.ps[6][:, :n], lhsT=BONESr, rhs=SQ[:, :n], start=True, stop=True),
                      reads=["sq%d" % q, "bones"], writes=["ps6"])
                P.add("act", lambda e: e.activation(out=xq(Qinv), in_=self.ps[6][:, :n], func=AF.Sqrt), reads=["ps6"], writes=["xq%d" % Qinv])
                P.add("dve", lambda e: e.tensor_scalar(out=xq(Qinv), in0=xq(Qinv), scalar1=1e-12, scalar2=None, op0=ALU.max),
                      reads=["xq%d" % Qinv], writes=["xq%d" % Qinv])
                P.add("dve", lambda e: e.reciprocal(out=xq(Qinv), in_=xq(Qinv)), reads=["xq%d" % Qinv], writes=["xq%d" % Qinv])
                P.add("dve", lambda e: e.tensor_tensor(out=xq(Qkk), in0=xq(Qkk), in1=xq(Qinv), op=ALU.mult),
                      reads=["xq%d" % Qkk, "xq%d" % Qinv], writes=["xq%d" % Qkk])
                for d in range(2):
                    Qt1 = Qt10 + d
                    p_z = self.next_ps(0, 6)
                    P.add("pe", lambda e, d=d, j=j, p_z=p_z: e.matmul(self.ps[p_z][:, :n], lhsT=L2W[0:96, d, j * 128:(j + 1) * 128],
                                                                     rhs=HL[0:96, d, :], start=True, stop=True),
                          reads=["l2w", "hl"], writes=["ps%d" % p_z])
                    p_a = self.next_ps(0, 6)
                    P.add("pe", lambda e, d=d, j=j, p_a=p_a: e.matmul(self.ps[p_a][:, :n], lhsT=L2W[0:96, 2 + d, j * 128:(j + 1) * 128],
                                                                     rhs=HL[0:96, 2 + d, :], start=True, stop=True),
                          reads=["l2w", "hl"], writes=["ps%d" % p_a])
                    P.add("act", lambda e, d=d, j=j, p_z=p_z: e.activation(out=xq(Qsig), in_=self.ps[p_z][:, :n], func=AF.Sigmoid,
                                                                          bias=self.pvc("w0%d" % d, j)),
                          reads=["ps%d" % p_z, "pv"], writes=["xq%d" % Qsig])
                    P.add("act", lambda e, d=d, j=j, p_a=p_a: e.activation(out=xq(Qa), in_=self.ps[p_a][:, :n], func=AF.Sigmoid,
                                                                          bias=self.pvc("a0%d" % d, j)),
                          reads=["ps%d" % p_a, "pv"], writes=["xq%d" % Qa])
                    P.add("dve", lambda e: e.tensor_tensor_scan(out=xq(Qcs), data0=MRESET, data1=xq(Qsig), initial=0.0,
                                                                op0=ALU.mult, op1=ALU.add),
                          reads=["cst", "xq%d" % Qsig], writes=["xq%d" % Qcs])
                    cs3 = xq(Qcs).rearrange("p (c t) -> p c t", t=CH)
                    P.add("act", lambda e, cs3=cs3: e.activation(out=xq(Qtot)[:, 0:nch], in_=cs3[:, :, CH - 1], func=AF.Exp, scale=-LWS),
                          reads=["xq%d" % Qcs], writes=["xq%d" % Qtot])
                    P.add("sp", lambda e, d=d, j=j, ch0=ch0: e.dma_start(out=self.DCD[d][j * 128:(j + 1) * 128, ch0:ch0 + nch],
                                                                       in_=xq(Qtot)[:, 0:nch]),
                          reads=["xq%d" % Qtot], writes=[], dma="dcst")
                    if d == 1:
                        P.add("dve", lambda e, cs3=cs3: e.tensor_tensor(
                            out=xq(Qcx).rearrange("p (c t) -> p c t", t=CH), in0=xq(Qsig).rearrange("p (c t) -> p c t", t=CH),
                            in1=cs3[:, :, CH - 1:CH].to_broadcast([128, nch, CH]), op=ALU.add),
                            reads=["xq%d" % Qcs, "xq%d" % Qsig], writes=["xq%d" % Qcx])
                        P.add("dve", lambda e: e.tensor_tensor(out=xq(Qcs), in0=xq(Qcx), in1=xq(Qcs), op=ALU.subtract),
                              reads=["xq%d" % Qcs, "xq%d" % Qcx], writes=["xq%d" % Qcs])
                    P.add("dve", lambda e: e.tensor_tensor(out=xq(Qcx), in0=xq(Qcs), in1=xq(Qsig), op=ALU.subtract),
                          reads=["xq%d" % Qcs, "xq%d" % Qsig], writes=["xq%d" % Qcx])
                    P.add("act", lambda e: e.activation(out=xq(Qer), in_=xq(Qcs), func=AF.Exp, scale=-LWS), reads=["xq%d" % Qcs], writes=["xq%d" % Qer])
                    P.add("act", lambda e: e.activation(out=xq(Qek), in_=xq(Qcs), func=AF.Exp, scale=LWS), reads=["xq%d" % Qcs], writes=["xq%d" % Qek])
                    P.add("act", lambda e: e.activation(out=xq(Qea), in_=xq(Qcx), func=AF.Exp, scale=-LWS), reads=["xq%d" % Qcx], writes=["xq%d" % Qea])
                    LB = xq(QL[d], 2).rearrange("p (c w) -> p c w", w=128)
                    RA = xq(QR[d], 2).rearrange("p (c w) -> p c w", w=128)
                    v3 = lambda k: xq(k).rearrange("p (c t) -> p c t", t=CH)
                    P.add("dve", lambda e, RA=RA: e.tensor_tensor(out=RA[:, :, CH:2 * CH], in0=v3(Qr), in1=v3(Qer), op=ALU.mult),
                          reads=["xq%d" % Qr, "xq%d" % Qer], writes=["xq%d" % QR[d]])
                    P.add("dve", lambda e, RA=RA: e.scalar_tensor_tensor(out=RA[:, :, 0:CH], in0=v3(Qkk), scalar=-1.0, in1=v3(Qea),
                                                                        op0=ALU.mult, op1=ALU.mult),
                          reads=["xq%d" % Qkk, "xq%d" % Qea], writes=["xq%d" % QR[d]])
                    P.add("dve", lambda e, j=j, Qt1=Qt1: e.tensor_scalar(out=xq(Qt1), in0=xq(Qa), scalar1=self.pvc("ka", j), scalar2=self.OMKA[:, j:j + 1],
                                                                        op0=ALU.mult, op1=ALU.add),
                          reads=["xq%d" % Qa, "pv", "omka"], writes=["xq%d" % Qt1])
                    P.add("dve", lambda e, Qt1=Qt1: e.tensor_tensor(out=xq(Qkd), in0=xq(Qk), in1=xq(Qt1), op=ALU.mult),
                          reads=["xq%d" % Qk, "xq%d" % Qt1], writes=["xq%d" % Qkd])
                    P.add("dve", lambda e, LB=LB: e.tensor_tensor(out=LB[:, :, CH:2 * CH], in0=v3(Qkd), in1=v3(Qek), op=ALU.mult),
                          reads=["xq%d" % Qkd, "xq%d" % Qek], writes=["xq%d" % QL[d]])
                    P.add("dve", lambda e: e.tensor_tensor(out=xq(Qt2), in0=xq(Qkk), in1=xq(Qa), op=ALU.mult),
                          reads=["xq%d" % Qkk, "xq%d" % Qa], writes=["xq%d" % Qt2])
                    P.add("dve", lambda e, LB=LB: e.tensor_tensor(out=LB[:, :, 0:CH], in0=v3(Qt2), in1=v3(Qek), op=ALU.mult),
                          reads=["xq%d" % Qt2, "xq%d" % Qek], writes=["xq%d" % QL[d]])
                    P.add("sp", lambda e, d=d, j=j, ch0=ch0, LB=LB: e.dma_start(out=self.LBKD[d][j * 128:(j + 1) * 128, ch0:ch0 + nch, :], in_=LB),
                          reads=["xq%d" % QL[d]], writes=[], dma="lbst%d" % d)
                    P.add("sp", lambda e, d=d, j=j, ch0=ch0, RA=RA: e.dma_start(out=self.RARD[d][j * 128:(j + 1) * 128, ch0:ch0 + nch, :], in_=RA),
                          reads=["xq%d" % QR[d]], writes=[], dma="rast%d" % d)
                if not is_ctx:
                    P.add("dve", lambda e: e.tensor_tensor(out=xq(Qt10), in0=xq(Qt10), in1=xq(Qt11), op=ALU.add),
                          reads=["xq%d" % Qt10, "xq%d" % Qt11], writes=["xq%d" % Qt10])
                    P.add("dve", lambda e: e.tensor_tensor(out=xq(Qt10), in0=xq(Qt10), in1=xq(Qk), op=ALU.mult),
                          reads=["xq%d" % Qt10, "xq%d" % Qk], writes=["xq%d" % Qt10])
                    PR = self.PRr
                    P.add("dve", lambda e, j=j, PR=PR: e.scalar_tensor_tensor(out=PR, in0=xq(Qt10), scalar=self.pvc("rk", j), in1=xq(Qr),
                                                                          op0=ALU.mult, op1=ALU.mult),
                          reads=["xq%d" % Qt10, "xq%d" % Qr, "pv"], writes=["xq%d" % Qpr])
                    for su in range(n // 128):
                        P.add("pe", lambda e, su=su, j=j, PR=PR: e.matmul(psB[:, su * NH + 2 * j:su * NH + 2 * j + 2],
                                                                      lhsT=PR[:, su * 128:(su + 1) * 128], rhs=HSELr, start=True, stop=True),
                              reads=["xq%d" % Qpr, "hsel"], writes=["ps7"])
            if not is_ctx:
                P.add("act", lambda e: e.copy(out=BONS, in_=psB[:, 0:2 * NH].rearrange("p (s h) -> p s h", h=NH)), reads=["ps7"], writes=["bons"])
                P.add("sp", lambda e, t0=t0: e.dma_start(out=self.BOND[t0:t0 + n, :].rearrange("(s p) h -> p s h", p=128), in_=BONS),
                      reads=["bons"], writes=[], dma="bonst")

    @contextlib.contextmanager
    def scan_bufs(self):
        nc = self.nc
        HQ = 8
        with contextlib.ExitStack() as st:
            sb = lambda name, shape, dt: st.enter_context(nc.sbuf_tensor(name, shape, dt)).ap()
            self.SC = []
            for q in range(2):
                t = {}
                t["lbkraw"] = [sb("lbkraw%d_%d" % (q, b), [64, HQ, 128], F32) for b in range(2)]
                t["rarraw"] = [sb("rarraw%d_%d" % (q, b), [64, HQ, 128], F32) for b in range(2)]
                t["uvraw"] = [sb("uvraw%d_%d" % (q, b), [128, HQ, 64], F32) for b in range(2)]
                t["dc"] = sb("dc%d" % q, [64, HQ, NCHK], F32)
                t["yt"] = [sb("yt%d_%d" % (q, b), [64, HQ, 64], F32) for b in range(2)]
                t["lbkr"] = sb("lbkr%d" % q, [64, HQ + 1, 128], F32R)
                t["rarr"] = sb("rarr%d" % q, [64, HQ + 1, 128], F32R)
                t["mb"] = sb("mb%d" % q, [128, HQ + 1, 128], F32R)
                t["nt"] = [sb("nt%d_%d" % (q, b), [64, HQ + 1, 64], F32R) for b in range(2)]
                t["nn"] = [sb("nn%d_%d" % (q, b), [64, HQ + 1, 64], F32R) for b in range(2)]
                t["zz"] = [sb("zz%d_%d" % (q, b), [64, HQ + 1, 64], F32R) for b in range(2)]
                t["xs"] = sb("xs%d" % q, [64, HQ, 64], F32R)
                t["ss"] = sb("ss%d" % q, [64, HQ, 64], F32R)
                t["uv"] = sb("uv%d" % q, [128, HQ, 64], F32R)
                t["uv0"] = sb("uv0%d" % q, [128, HQ, 64], F32R)
                t["lbkt"] = sb("lbkt%d" % q, [128, HQ + 1, 64], F32R)
                self.SC.append(t)
            yield

    def rwkv_scan(self, only_hq=None):
        only_hq = getattr(self, 'only_hq', None)
        for hq in range(4):
            if only_hq is not None and hq not in only_hq:
                continue
            gens = [self.scan_stream(q, hq, q) for q in range(2)]
            alive = list(gens)
            nsteps = 0
            maxsteps = int(os.environ.get("SCAN_STEPS", "0")) or None
            while alive:
                if maxsteps is not None and nsteps >= maxsteps:
                    break
                nsteps += 1
                for g in list(alive):
                    try:
                        next(g)
                    except StopIteration:
                        alive.remove(g)

    def scan_stream(self, q, hq, d):
        P = self.P
        HQ = 8
        tl = self.SC[q]
        K = lambda name: "s%d_%s" % (q, name)
        cst = self.CST
        MASK = cst[:, (C_MASKF if d == 0 else C_MASKB):(C_MASKF if d == 0 else C_MASKB) + 128]
        MT = cst[0:64, (C_MTF if d == 0 else C_MTB):(C_MTF if d == 0 else C_MTB) + 64]
        ID64 = cst[0:64, C_ID:C_ID + 64]
        IDr = self.IDr[0:64, :]
        LBv = self.LBKD[d].rearrange("(h p) c w -> p h c w", p=64)
        RAv = self.RARD[d].rearrange("(h p) c w -> p h c w", p=64)
        DCv = self.DCD[d].rearrange("(h p) c -> p h c", p=64)
        hs = slice(hq * HQ, (hq + 1) * HQ)
        order = list(range(NCHK)) if d == 0 else [3, 2, 1, 0] + list(range(NCHK - 1, 3, -1))
        if getattr(self, "scan_limit", None):
            order = order[:self.scan_limit]
        banks = [4 * q + b for b in range(4)]
        bstate = [0]

        def bank():
            b = banks[bstate[0] % 4]
            bstate[0] += 1
            return b

        f32 = lambda ap: ap.bitcast(F32)

        def ext(tile, hl, c0):
            w = tile.shape[2]
            return tile.rearrange("p h w -> p (h w)")[:, hl * w + c0:hl * w + c0 + 128]
        if hq == 0 or self.only_hq is not None:
            pads = [("lbkr", tl["lbkr"]), ("rarr", tl["rarr"]), ("mb", tl["mb"]), ("lbkt", tl["lbkt"])]
            pads += [("nt%d" % b, tl["nt"][b]) for b in range(2)] + [("nn%d" % b, tl["nn"][b]) for b in range(2)]
            pads += [("zz%d" % b, tl["zz"][b]) for b in range(2)]
            for nm, tile in pads:
                np_, w = tile.shape[0], tile.shape[2]
                P.add("dve", lambda e, tile=tile, np_=np_, w=w: e.tensor_copy(out=tile[:, HQ, :], in_=self.ZERO[0:np_, 0:w]),
                      reads=["zero"], writes=[K(nm)])
        if hq == 0 or self.only_hq is not None:
            P.add("dve", lambda e: e.tensor_copy(out=tl["uv0"][0:64, :, :], in_=self.ZERO[0:64, :].rearrange("p (h v) -> p h v", v=64)),
                  reads=["zero"], writes=[K("uv0")])
        P.add("dve", lambda e: e.tensor_copy(out=tl["ss"], in_=self.ZERO[0:64, :].rearrange("p (h v) -> p h v", v=64)),
              reads=["zero"], writes=[K("ss")])
        P.add("sp", lambda e: e.dma_start(out=tl["dc"], in_=DCv[:, hs, :]), writes=[K("dc")], dma=K("dc"))

        def loads(idx):
            ch = order[idx]
            b = idx % 2
            P.add("sp", lambda e: e.dma_start(out=tl["lbkraw"][b], in_=LBv[:, hs, ch, :]), writes=[K("lbkraw%d" % b)], dma=K("lbk%d" % b))
            P.add("sp", lambda e: e.dma_start(out=tl["rarraw"][b], in_=RAv[:, hs, ch, :]), writes=[K("rarraw%d" % b)], dma=K("rar%d" % b))
            P.add("sp", lambda e: e.dma_start(out=tl["uvraw"][b][64:128, :, :],
                                              in_=self.VD[ch * CH:(ch + 1) * CH, hq * 512:(hq + 1) * 512].rearrange("t (h v) -> t h v", v=64)),
                  writes=[K("uvraw%d" % b)], dma=K("uvr%d" % b))

        loads(0)
        for idx, ch in enumerate(order):
            b = idx % 2
            if idx + 1 < len(order):
                loads(idx + 1)
            lbkraw, rarraw, uvraw = tl["lbkraw"][b], tl["rarraw"][b], tl["uvraw"][b]
            lbkr, rarr, mb, uv, lbkt, xs, ss = tl["lbkr"], tl["rarr"], tl["mb"], tl["uv"], tl["lbkt"], tl["xs"], tl["ss"]
            P.add("act", lambda e, lbkraw=lbkraw: e.copy(out=lbkr[:, 0:HQ, :], in_=lbkraw), reads=[K("lbkraw%d" % b)], writes=[K("lbkr")])
            P.add("dve", lambda e, rarraw=rarraw: e.tensor_copy(out=rarr[:, 0:HQ, :], in_=rarraw), reads=[K("rarraw%d" % b)], writes=[K("rarr")])
            P.add("act", lambda e, uvraw=uvraw: e.copy(out=uv[64:128, :, :], in_=uvraw[64:128, :, :]), reads=[K("uvraw%d" % b)], writes=[K("uvv")])
            P.add("dve", lambda e, uvraw=uvraw: e.tensor_copy(out=tl["uv0"][64:128, :, :], in_=uvraw[64:128, :, :]), reads=[K("uvraw%d" % b)], writes=[K("uv0")])
            pm = [bank(), bank()]
            for hl in range(HQ):
                P.add("pe", lambda e, hl=hl, pm=pm: e.matmul(self.ps[pm[hl // 4]][:, (hl % 4) * 128:(hl % 4 + 1) * 128],
                                                            lhsT=lbkr[:, hl, :], rhs=rarr[:, hl, :], start=True, stop=True),
                      reads=[K("lbkr"), K("rarr")], writes=["ps%d" % pm[hl // 4]])
            for g in range(2):
                P.add("dve", lambda e, g=g, pm=pm: e.tensor_tensor(
                    out=mb[:, 4 * g:4 * g + 4, :], in0=self.ps[pm[g]].rearrange("p (h w) -> p h w", w=128),
                    in1=MASK.unsqueeze(1).to_broadcast([128, 4, 128]), op=ALU.mult),
                    reads=["ps%d" % pm[g], "cst"], writes=[K("mb")])
            pn = bank()
            for hl in range(HQ):
                P.add("pe", lambda e, hl=hl, pn=pn: e.matmul(self.ps[pn][:, hl * 64:(hl + 1) * 64],
                                                            lhsT=rarr[:, hl, :], rhs=lbkr[:, hl, 0:64], start=True, stop=True),
                      reads=[K("lbkr"), K("rarr")], writes=["ps%d" % pn])
            P.add("dve", lambda e, pn=pn: e.tensor_tensor(
                out=tl["nt"][0][:, 0:HQ, :], in0=self.ps[pn][0:64, :].rearrange("p (h w) -> p h w", w=64),
                in1=MT.unsqueeze(1).to_broadcast([64, HQ, 64]), op=ALU.mult),
                reads=["ps%d" % pn, "cst"], writes=[K("nt0")])
            pt = bank()
            for hl in range(HQ):
                P.add("pe", lambda e, hl=hl, pt=pt, lbkraw=lbkraw: e.transpose(self.ps[pt][:, hl * 64:(hl + 1) * 64], lbkraw[:, hl, :], ID64),
                      reads=[K("lbkraw%d" % b), "cst"], writes=["ps%d" % pt])
            P.add("act", lambda e, pt=pt: e.copy(out=lbkt[:, 0:HQ, :], in_=self.ps[pt].rearrange("p (h w) -> p h w", w=64)),
                  reads=["ps%d" % pt], writes=[K("lbkt")])
            P.add("dve", lambda e: e.tensor_tensor(out=tl["zz"][0][:, 0:HQ, :], in0=f32(mb[0:64, 0:HQ, 0:64]),
                                                   in1=ID64.unsqueeze(1).to_broadcast([64, HQ, 64]), op=ALU.add),
                  reads=[K("mb"), "cst"], writes=[K("zz0")])
            yield
            for l in range(1, 7):
                evs = []
                if l <= 5:
                    ntp, nto = tl["nt"][(l - 1) % 2], tl["nt"][l % 2]
                    kntp, knto = K("nt%d" % ((l - 1) % 2)), K("nt%d" % (l % 2))
                    if l == 1:
                        nnp_fn = lambda hl: mb[0:64, hl, 0:64]
                        nnl_fn = lambda hl: mb[0:64, hl, :]
                        knnp = K("mb")
                    else:
                        nnp_t = tl["nn"][(l - 1) % 2]
                        nnp_fn = lambda hl, nnp_t=nnp_t: nnp_t[:, hl, :]
                        nnl_fn = lambda hl, nnp_t=nnp_t: ext(nnp_t, hl, 0)
                        knnp = K("nn%d" % ((l - 1) % 2))
                    if l <= 4:
                        pa = bank()
                        for hl in range(HQ):
                            P.add("pe", lambda e, hl=hl, pa=pa, ntp=ntp, nnp_fn=nnp_fn: e.matmul(
                                self.ps[pa][:, hl * 64:(hl + 1) * 64], lhsT=ext(ntp, hl, 0), rhs=nnp_fn(hl), start=True, stop=True),
                                reads=[kntp, knnp], writes=["ps%d" % pa])
                        evs.append(("act", pa, tl["nn"][l % 2], K("nn%d" % (l % 2))))
                    pb = bank()
                    for hl in range(HQ):
                        P.add("pe", lambda e, hl=hl, pb=pb, ntp=ntp, nnl_fn=nnl_fn: e.matmul(
                            self.ps[pb][:, hl * 64:(hl + 1) * 64], lhsT=nnl_fn(hl), rhs=ntp[:, hl, :], start=True, stop=True),
                            reads=[kntp, knnp], writes=["ps%d" % pb])
                    evs.append(("dve", pb, nto, knto))
                if l >= 2:
                    m = l - 1
                    ntm, kntm = tl["nt"][m % 2], K("nt%d" % (m % 2))
                    zp, kzp = tl["zz"][(m - 1) % 2], K("zz%d" % ((m - 1) % 2))
                    pz = bank()
                    for hl in range(HQ):
                        P.add("pe", lambda e, hl=hl, pz=pz, ntm=ntm, zp=zp: e.matmul(
                            self.ps[pz][:, hl * 64:(hl + 1) * 64], lhsT=ext(ntm, hl, 0), rhs=zp[:, hl, :], start=True, stop=False),
                            reads=[kntm, kzp], writes=["ps%d" % pz])
                        P.add("pe", lambda e, hl=hl, pz=pz, zp=zp: e.matmul(
                            self.ps[pz][:, hl * 64:(hl + 1) * 64], lhsT=IDr, rhs=zp[:, hl, :], start=False, stop=True),
                            reads=["idr", kzp], writes=["ps%d" % pz])
                    evs.append(("act" if l % 2 == 0 else "dve", pz, tl["zz"][m % 2], K("zz%d" % (m % 2))))
                for (eng, pi, dst, dkey) in evs:
                    src = self.ps[pi][0:64, :].rearrange("p (h w) -> p h w", w=64)
                    if eng == "act":
                        P.add("act", lambda e, src=src, dst=dst: e.copy(out=dst[:, 0:HQ, :], in_=src), reads=["ps%d" % pi], writes=[dkey])
                    else:
                        P.add("dve", lambda e, src=src, dst=dst: e.tensor_copy(out=dst[:, 0:HQ, :], in_=src), reads=["ps%d" % pi], writes=[dkey])
                yield
            TT = tl["zz"][5 % 2]
            kT = K("zz%d" % (5 % 2))
            px = bank()
            for hl in range(HQ):
                P.add("pe", lambda e, hl=hl, px=px: e.matmul(self.ps[px][:, hl * 64:(hl + 1) * 64], lhsT=rarr[:, hl, :], rhs=ss[:, hl, :],
                                                            start=True, stop=False),
                      reads=[K("rarr"), K("ss")], writes=["ps%d" % px])
                P.add("pe", lambda e, hl=hl, px=px: e.matmul(self.ps[px][:, hl * 64:(hl + 1) * 64], lhsT=mb[:, hl, :], rhs=tl["uv0"][:, hl, :],
                                                            start=False, stop=True),
                      reads=[K("mb"), K("uv0")], writes=["ps%d" % px])
            P.add("act", lambda e, px=px: e.copy(out=xs, in_=self.ps[px][0:64, :].rearrange("p (h w) -> p h w", w=64)),
                  reads=["ps%d" % px], writes=[K("xs")])
            yield
            pu = bank()
            for hl in range(HQ):
                P.add("pe", lambda e, hl=hl, pu=pu, TT=TT: e.matmul(self.ps[pu][:, hl * 64:(hl + 1) * 64], lhsT=ext(TT, hl, 0), rhs=xs[:, hl, :],
                                                                   start=True, stop=True),
                      reads=[kT, K("xs")], writes=["ps%d" % pu])
            P.add("dve", lambda e, pu=pu: e.tensor_copy(out=uv[0:64, :, :], in_=self.ps[pu][0:64, :].rearrange("p (h w) -> p h w", w=64)),
                  reads=["ps%d" % pu], writes=[K("uvu")])
            yield
            py = bank()
            pS = bank()
            for hl in range(HQ):
                P.add("pe", lambda e, hl=hl, py=py: e.matmul(self.ps[py][:, hl * 64:(hl + 1) * 64], lhsT=ext(rarr, hl, 64), rhs=ss[:, hl, :],
                                                            start=True, stop=False),
                      reads=[K("rarr"), K("ss")], writes=["ps%d" % py])
                P.add("pe", lambda e, hl=hl, py=py: e.matmul(self.ps[py][:, hl * 64:(hl + 1) * 64], lhsT=ext(mb, hl, 64), rhs=uv[:, hl, :],
                                                            start=False, stop=True),
                      reads=[K("mb"), K("uvu"), K("uvv")], writes=["ps%d" % py])
            for hl in range(HQ):
                P.add("pe", lambda e, hl=hl, pS=pS: e.matmul(self.ps[pS][:, hl * 64:(hl + 1) * 64], lhsT=IDr, rhs=ss[:, hl, :],
                                                            start=True, stop=False),
                      reads=["idr", K("ss")], writes=["ps%d" % pS])
                P.add("pe", lambda e, hl=hl, pS=pS: e.matmul(self.ps[pS][:, hl * 64:(hl + 1) * 64], lhsT=ext(lbkt, hl, 0), rhs=uv[:, hl, :],
                                                            start=False, stop=True),
                      reads=[K("lbkt"), K("uvu"), K("uvv")], writes=["ps%d" % pS])
            if ch >= TC // CH:
                yt = tl["yt"][b]
                P.add("act", lambda e, py=py, yt=yt: e.copy(out=yt, in_=self.ps[py][0:64, :].rearrange("p (h w) -> p h w", w=64)),
                      reads=["ps%d" % py], writes=[K("yt%d" % b)])
                tx = (ch - TC // CH) * CH
                P.add("sp", lambda e, yt=yt, tx=tx: e.dma_start(
                    out=self.YD[d][tx:tx + CH, hq * 512:(hq + 1) * 512].rearrange("t (h v) -> t h v", v=64), in_=yt),
                    reads=[K("yt%d" % b)], writes=[], dma=K("yst%d" % b))
            P.add("dve", lambda e, pS=pS, ch=ch: e.tensor_tensor(
                out=ss, in0=self.ps[pS][0:64, :].rearrange("p (h w) -> p h w", w=64),
                in1=tl["dc"][:, :, ch:ch + 1].to_broadcast([64, HQ, 64]), op=ALU.mult),
                reads=["ps%d" % pS, K("dc")], writes=[K("ss")])
            yield

    def rwkv_readout(self):
        P = self.P
        n = 512
        BG = self.BIGA[:, 0:24704].bitcast(F32)
        Y0 = BG[:, 0:2048]
        Y1 = BG[:, 2048:4096]
        VV = BG[:, 4096:6144]
        GG = BG[:, 6144:8192]
        OB = self.BIGA[:, 16384:18432]
        RW = self.RWB.bitcast(F32)
        LNW = RW[:, 0:2048]
        LNB = RW[:, 2048:4096]
        BON = RW[:, 4096:4128]
        MEAN = RW[:, 4128:4160]
        RSTD = RW[:, 4160:4192]
        h3 = lambda ap: ap.rearrange("p (h v) -> p h v", v=64)
        bc = lambda ap: ap.unsqueeze(2).to_broadcast([128, NH, 64])
        XTv = self.XT.rearrange("(c p) t -> p c t", p=128)
        xTv = self.xT.rearrange("(c p) t -> p c t", p=128)
        wo = self.rw_wo.rearrange("(kc p) n -> p kc n", p=128)
        P.add("sp", lambda e: e.dma_start(out=LNW, in_=self.lnwb[0]), writes=["lnw"], dma="lnw")
        P.add("sp", lambda e: e.dma_start(out=LNB, in_=self.lnwb[1]), writes=["lnb"], dma="lnb")
        for t in range(T // n):
            t0 = t * n
            P.add("sp", lambda e, t0=t0: e.dma_start(out=self.XTB, in_=xTv[:, :, t0:t0 + n]), writes=["xtb"], dma="xtb")
            for su in range(4):
                r0 = t0 + su * 128
                P.add("sp", lambda e, r0=r0: e.dma_start(out=Y0, in_=self.YD[0][r0:r0 + 128, :]), writes=["y0"], dma="y0")
                P.add("sp", lambda e, r0=r0: e.dma_start(out=Y1, in_=self.YD[1][r0:r0 + 128, :]), writes=["y1"], dma="y1")
                P.add("sp", lambda e, r0=r0: e.dma_start(out=VV, in_=self.VD[TC + r0:TC + r0 + 128, :]), writes=["vv"], dma="vv")
                P.add("sp", lambda e, r0=r0: e.dma_start(out=GG, in_=self.GD[r0:r0 + 128, :]), writes=["gg"], dma="gg")
                P.add("sp", lambda e, r0=r0: e.dma_start(out=BON, in_=self.BOND[r0:r0 + 128, :]), writes=["bon"], dma="bon")
                P.add("dve", lambda e: e.tensor_tensor(out=Y0, in0=Y0, in1=Y1, op=ALU.add), reads=["y0", "y1"], writes=["y0"])
                P.add("dve", lambda e: e.reduce_sum(out=MEAN, in_=h3(Y0), axis=AX.X), reads=["y0"], writes=["mean"])
                P.add("dve", lambda e: e.tensor_scalar(out=MEAN, in0=MEAN, scalar1=1.0 / 64, scalar2=None, op0=ALU.mult), reads=["mean"], writes=["mean"])
                P.add("dve", lambda e: e.tensor_tensor(out=h3(Y0), in0=h3(Y0), in1=bc(MEAN), op=ALU.subtract), reads=["y0", "mean"], writes=["y0"])
                P.add("act", lambda e: e.activation(out=Y1, in_=Y0, func=AF.Square), reads=["y0"], writes=["y1"])
                P.add("dve", lambda e: e.reduce_sum(out=RSTD, in_=h3(Y1), axis=AX.X), reads=["y1"], writes=["rstd"])
                P.add("act", lambda e: e.activation(out=RSTD, in_=RSTD, func=AF.Sqrt, scale=1.0 / 64, bias=GN_EPS), reads=["rstd"], writes=["rstd"])
                P.add("dve", lambda e: e.reciprocal(out=RSTD, in_=RSTD), reads=["rstd"], writes=["rstd"])
                P.add("dve", lambda e: e.tensor_tensor(out=h3(Y0), in0=h3(Y0), in1=bc(RSTD), op=ALU.mult), reads=["y0", "rstd"], writes=["y0"])
                P.add("dve", lambda e: e.tensor_tensor(out=Y0, in0=Y0, in1=LNW, op=ALU.mult), reads=["y0", "lnw"], writes=["y0"])
                P.add("dve", lambda e: e.tensor_tensor(out=Y0, in0=Y0, in1=LNB, op=ALU.add), reads=["y0", "lnb"], writes=["y0"])
                P.add("dve", lambda e: e.tensor_tensor(out=h3(VV), in0=h3(VV), in1=bc(BON), op=ALU.mult), reads=["vv", "bon"], writes=["vv"])
                P.add("dve", lambda e: e.tensor_tensor(out=Y0, in0=Y0, in1=VV, op=ALU.add), reads=["y0", "vv"], writes=["y0"])
                P.add("dve", lambda e: e.tensor_tensor(out=OB, in0=Y0, in1=GG, op=ALU.mult), reads=["y0", "gg"], writes=["ob"])
                for cg in range(4):
                    pi = self.next_ps(0, 6)
                    psb = self.ps[pi].bitcast(BF16)
                    for cc in range(4):
                        c = cg * 4 + cc
                        P.add("pe", lambda e, c=c, cc=cc, psb=psb: e.transpose(psb[:, cc * 128:(cc + 1) * 128], OB[:, c * 128:(c + 1) * 128], self.IDb),
                              reads=["ob", "idb"], writes=["ps%d" % pi])
                    P.add("act", lambda e, cg=cg, su=su, psb=psb: e.copy(
                        out=self.HB[:, cg * 4:(cg + 1) * 4, su * 128:(su + 1) * 128], in_=psb[:, 0:512].rearrange("p (c t) -> p c t", t=128)),
                        reads=["ps%d" % pi], writes=["hb"])
            self.proj_residual(wo, KC, lambda kc: self.HB[:, kc, :n], ["hb"], self.MS[0][:, 2, :], "ms0", n)
            P.add("sp", lambda e, t0=t0: e.dma_start(out=XTv[:, :, t0:t0 + n], in_=self.XTB),
                  reads=["xtb"], writes=["XT%d" % t], dma="xtbo")

    def layer1_mixer(self):
        P = self.P
        MS = self.MS[1]
        n = 512
        XTv = self.XT.rearrange("(c p) t -> p c t", p=128)
        ZDv = self.ZD.rearrange("(c p) t -> p c t", p=128)
        GBv = self.GBD.rearrange("(c p) t -> p c t", p=128)
        Z = self.BIGA[:, 0:16384].bitcast(F32).rearrange("p (c t) -> p c t", t=512)
        GB = self.BIGA[:, 16384:24576].rearrange("p (c t) -> p c t", t=512)
        win = self.sc_win.rearrange("(kc p) n -> p kc n", p=128)
        P.add("sp", lambda e: e.dma_start(out=ZDv[:, :, 0:1], in_=self.ZERO[:, 0:16].unsqueeze(2), allow_slow_non_contiguous=True), reads=["zero"], writes=["ZDpad"], dma="zd0")
        P.add("sp", lambda e: e.dma_start(out=ZDv[:, :, T + 1:T + 2], in_=self.ZERO[:, 0:16].unsqueeze(2), allow_slow_non_contiguous=True), reads=["zero"], writes=["ZDpad"], dma="zd0")
        for t in range(T // n):
            t0 = t * n
            P.add("sp", lambda e, t0=t0: e.dma_start(out=self.XTB, in_=XTv[:, :, t0:t0 + n]),
                  reads=["XT%d" % t], writes=["xtb"], dma="xtb")
            self.norm_mod(self.XTB, "xtb", MS[:, 0, :], MS[:, 1, :], "ms1", self.HB, "hb", n)
            for nb in range(4):
                slots = []
                for part in range(3):
                    slots.append(self.load_w(win[:, :, part * D + nb * 512: part * D + (nb + 1) * 512], 128, KC, 512))
                for jj in range(4):
                    j = nb * 4 + jj
                    pss = []
                    for part in range(3):
                        s, view = slots[part]
                        pi = self.next_ps(0, 6)
                        pss.append(pi)
                        for kc in range(KC):
                            P.add("pe", lambda e, view=view, pi=pi, kc=kc, jj=jj: e.matmul(
                                self.ps[pi][:, :n], lhsT=view[:, kc, jj * 128:(jj + 1) * 128], rhs=self.HB[:, kc, :n],
                                start=(kc == 0), stop=(kc == KC - 1)),
                                reads=["ws%d" % s, "hb"], writes=["ps%d" % pi])
                    q = self.tmp_i % 3
                    self.tmp_i += 1
                    TM = self.TMP[q]
                    P.add("act", lambda e, j=j, pi=pss[0]: e.copy(out=GB[:, j, :], in_=self.ps[pi][:, :n]),
                          reads=["ps%d" % pss[0]], writes=["bg1"])
                    P.add("act", lambda e, TM=TM, pi=pss[1]: e.copy(out=TM[:, :n], in_=self.ps[pi][:, :n]),
                          reads=["ps%d" % pss[1]], writes=["tmp%d" % q])
                    P.add("dve", lambda e, TM=TM, j=j, pi=pss[2]: e.tensor_tensor(
                        out=Z[:, j, :], in0=TM[:, :n], in1=self.ps[pi][:, :n], op=ALU.mult),
                        reads=["ps%d" % pss[2], "tmp%d" % q], writes=["bg0"])
            P.add("sp", lambda e, t0=t0: e.dma_start(out=ZDv[:, :, t0 + 1:t0 + 1 + n], in_=Z), reads=["bg0"], writes=["ZD%d" % t], dma="zst")
            P.add("sp", lambda e, t0=t0: e.dma_start(out=GBv[:, :, t0:t0 + n], in_=GB), reads=["bg1"], writes=["GBD%d" % t], dma="gbst")
        ZE = self.BIGA[:, 0:16 * 514 * 2].bitcast(F32).rearrange("p (c t) -> p c t", t=514)
        GB2 = self.BIGA[:, 16448:16448 + 8192].rearrange("p (c t) -> p c t", t=512)
        wout = self.sc_wout.rearrange("(kc p) n -> p kc n", p=128)
        for t in range(T // n):
            t0 = t * n
            P.add("sp", lambda e, t0=t0: e.dma_start(out=ZE, in_=ZDv[:, :, t0:t0 + n + 2]),
                  reads=["ZDpad"] + ["ZD%d" % u for u in (t - 1, t, t + 1) if 0 <= u < T // n], writes=["bg0", "bg1"], dma="zld")
            P.add("sp", lambda e, t0=t0: e.dma_start(out=GB2, in_=GBv[:, :, t0:t0 + n]),
                  reads=["GBD%d" % t], writes=["bg1"], dma="gbld")
            P.add("sp", lambda e, t0=t0: e.dma_start(out=self.XTB, in_=XTv[:, :, t0:t0 + n]),
                  reads=["XT%d" % t], writes=["xtb"], dma="xtb")
            for c in range(KC):
                q = self.tmp_i % 3
                self.tmp_i += 1
                TM = self.TMP[q]
                P.add("dve", lambda e, TM=TM, c=c: e.tensor_scalar(
                    out=TM[:, :n], in0=ZE[:, c, 0:n], scalar1=self.pvc("cw0", c), scalar2=None, op0=ALU.mult),
                    reads=["bg0", "bg1", "pv"], writes=["tmp%d" % q])
                P.add("dve", lambda e, TM=TM, c=c: e.scalar_tensor_tensor(
                    out=TM[:, :n], in0=ZE[:, c, 1:n + 1], scalar=self.pvc("cw1", c), in1=TM[:, :n], op0=ALU.mult, op1=ALU.add),
                    reads=["bg0", "bg1", "pv", "tmp%d" % q], writes=["tmp%d" % q])
                P.add("dve", lambda e, TM=TM, c=c: e.scalar_tensor_tensor(
                    out=TM[:, :n], in0=ZE[:, c, 2:n + 2], scalar=self.pvc("cw2", c), in1=TM[:, :n], op0=ALU.mult, op1=ALU.add),
                    reads=["bg0", "bg1", "pv", "tmp%d" % q], writes=["tmp%d" % q])
                P.add("dve", lambda e, TM=TM, c=c: e.tensor_tensor(
                    out=self.HB[:, c, :n], in0=TM[:, :n], in1=GB2[:, c, :], op=ALU.mult),
                    reads=["bg1", "tmp%d" % q], writes=["hb"])
            self.proj_residual(wout, KC, lambda kc: self.HB[:, kc, :n], ["hb"], MS[:, 2, :], "ms1", n)
            P.add("sp", lambda e, t0=t0: e.dma_start(out=XTv[:, :, t0:t0 + n], in_=self.XTB),
                  reads=["xtb"], writes=["XT%d" % t], dma="xtbo")

    def proj_residual(self, wsrc, nk, rhs_fn, rkeys, G, gkey, n):
        P = self.P
        wcols = 512 if nk <= 16 else 128
        for nb in range(D // wcols):
            s, view = self.load_w(wsrc[:, :, nb * wcols:(nb + 1) * wcols], 128, nk, wcols)
            for jj in range(wcols // 128):
                j = nb * (wcols // 128) + jj
                pi = self.next_ps(0, 6)
                for kc in range(nk):
                    P.add("pe", lambda e, view=view, pi=pi, kc=kc, jj=jj: e.matmul(
                        self.ps[pi][:, :n], lhsT=view[:, kc, jj * 128:(jj + 1) * 128], rhs=rhs_fn(kc),
                        start=(kc == 0), stop=(kc == nk - 1)),
                        reads=["ws%d" % s] + rkeys, writes=["ps%d" % pi])
                P.add("dve", lambda e, pi=pi, j=j: e.scalar_tensor_tensor(
                    out=self.XTB[:, j, :n], in0=self.ps[pi][:, :n], scalar=G[:, j:j + 1], in1=self.XTB[:, j, :n],
                    op0=ALU.mult, op1=ALU.add),
                    reads=["ps%d" % pi, gkey, "xtb"], writes=["xtb"])

    def ffn(self, i, final=False):
        P = self.P
        MS = self.MS[i]
        n = 512
        XTv = self.XT.rearrange("(c p) t -> p c t", p=128)
        OTv = self.outT.rearrange("(c p) t -> p c t", p=128)
        w13 = self.ffn_w13[i].rearrange("(kc p) n -> p kc n", p=128)
        w2 = self.ffn_w2[i].rearrange("(fc p) n -> p fc n", p=128)
        ACTT = self.BIGA[:, 0:NFC * 512].rearrange("p (f t) -> p f t", t=512)
        for t in range(T // n):
            t0 = t * n
            P.add("sp", lambda e, t0=t0: e.dma_start(out=self.XTB, in_=XTv[:, :, t0:t0 + n]),
                  reads=["XT%d" % t], writes=["xtb"], dma="xtb")
            self.norm_mod(self.XTB, "xtb", MS[:, 3, :], MS[:, 4, :], "ms%d" % i, self.HB, "hb", n)
            for fb in range(FF // 512):
                sa, va = self.load_w(w13[:, :, fb * 512:(fb + 1) * 512], 128, KC, 512)
                sb_, vb = self.load_w(w13[:, :, FF + fb * 512:FF + (fb + 1) * 512], 128, KC, 512)
                for jj in range(4):
                    f = fb * 4 + jj
                    pa = self.next_ps(0, 6)
                    pb = self.next_ps(0, 6)
                    for (pi, s, view) in ((pa, sa, va), (pb, sb_, vb)):
                        for kc in range(KC):
                            P.add("pe", lambda e, view=view, pi=pi, kc=kc, jj=jj: e.matmul(
                                self.ps[pi][:, :n], lhsT=view[:, kc, jj * 128:(jj + 1) * 128], rhs=self.HB[:, kc, :n],
                                start=(kc == 0), stop=(kc == KC - 1)),
                                reads=["ws%d" % s, "hb"], writes=["ps%d" % pi])
                    q = self.tmp_i % 3
                    self.tmp_i += 1
                    TM = self.TMP[q]
                    P.add("act", lambda e, TM=TM, pa=pa: e.activation(out=TM[:, :n], in_=self.ps[pa][:, :n], func=AF.Silu),
                          reads=["ps%d" % pa], writes=["tmp%d" % q])
                    P.add("dve", lambda e, TM=TM, pb=pb, f=f: e.tensor_tensor(
                        out=ACTT[:, f, :n], in0=TM[:, :n], in1=self.ps[pb][:, :n], op=ALU.mult),
                        reads=["ps%d" % pb, "tmp%d" % q], writes=["bg0", "bg1"])
            self.proj_residual(w2, NFC, lambda fc: ACTT[:, fc, :n], ["bg0", "bg1"], MS[:, 5, :], "ms%d" % i, n)
            if final:
                FO = self.BIGA[:, 0:16384].bitcast(F32).rearrange("p (c t) -> p c t", t=512)
                self.norm_mod(self.XTB, "xtb", self.pvc("fg"), None, "pv", FO, "bg0", n)
                P.add("sp", lambda e, t0=t0, FO=FO: e.dma_start(out=OTv[:, :, t0:t0 + n], in_=FO),
                      reads=["bg0"], writes=["outT%d" % t], dma="outst")
                if self.dbg:
                    P.add("sp", lambda e, t0=t0: e.dma_start(out=XTv[:, :, t0:t0 + n], in_=self.XTB),
                          reads=["xtb"], writes=["XT%d" % t], dma="xtbo")
            else:
                P.add("sp", lambda e, t0=t0: e.dma_start(out=XTv[:, :, t0:t0 + n], in_=self.XTB),
                      reads=["xtb"], writes=["XT%d" % t], dma="xtbo")


def make_in_maps(inp, cores):
    pv = pack_pvec(inp)
    shared = {
        "pvec": pv,
        "ada_w": np.ascontiguousarray(inp["ada_w"], np.float32),
        "sc_win": np.ascontiguousarray(inp["sc_win"][0], np.float32),
        "sc_wout": np.ascontiguousarray(inp["sc_wout"][0], np.float32),
        "ffn_w13": np.ascontiguousarray(inp["ffn_w13"], np.float32),
        "ffn_w2": np.ascontiguousarray(inp["ffn_w2"], np.float32),
        "rw_wr": np.ascontiguousarray(inp["rw_wr"][0], np.float32),
        "rw_wk": np.ascontiguousarray(inp["rw_wk"][0], np.float32),
        "rw_wv": np.ascontiguousarray(inp["rw_wv"][0], np.float32),
        "rw_wo": np.ascontiguousarray(inp["rw_wo"][0], np.float32),
        "rw_w1": np.ascontiguousarray(inp["rw_w1"][0], np.float32),
        "rw_w2": np.ascontiguousarray(inp["rw_w2"][0], np.float32),
        "rw_a1": np.ascontiguousarray(inp["rw_a1"][0], np.float32),
        "rw_a2": np.ascontiguousarray(inp["rw_a2"][0], np.float32),
        "rw_g1": np.ascontiguousarray(inp["rw_g1"][0], np.float32),
        "rw_g2": np.ascontiguousarray(inp["rw_g2"][0], np.float32),
        "lnwb": np.ascontiguousarray(np.stack([np.broadcast_to(inp["rw_lnw"][0], (128, D)),
                                               np.broadcast_to(inp["rw_lnb"][0], (128, D))]), np.float32),
        "cst": make_cst(),
    }
    maps = []
    for b in cores:
        m = dict(shared)
        m["xT"] = np.ascontiguousarray(np.asarray(inp["x"][b], np.float32).T)
        m["ctxT"] = np.ascontiguousarray(np.asarray(inp["ctx"][b], np.float32).T)
        cond = np.stack([_fm(inp["c"][b]), _fm(inp["c_ctx"])], axis=-1)
        m["cond"] = np.ascontiguousarray(cond, np.float32)
        maps.append(m)
    return maps


def kernel(**inputs):
    inp = {k: np.asarray(v) for k, v in inputs.items()}
    bld = Builder(start_layer=0)
    maps = make_in_maps(inp, list(range(8)))
    res = run_bass_kernel_spmd(bld.nc, maps, core_ids=list(range(8)))
    out = np.stack([np.ascontiguousarray(r["outT"].T) for r in res.results], axis=0)
    return out.astype(np.float32)
```

```python
import contextlib
import os
import numpy as np
import concourse.bass as bass
import concourse.mybir as mybir
from concourse.bass_utils import run_bass_kernel_spmd

F32 = mybir.dt.float32
BF16 = mybir.dt.bfloat16
F32R = mybir.dt.float32r
AF = mybir.ActivationFunctionType
ALU = mybir.AluOpType
AX = mybir.AxisListType

D = 2048
T = 2048
TC = 256
FF = 5632
KC = 16
NH = 32
CH = 64
NFC = FF // 128
TA = TC + T
NCHK = TA // CH
LWS = 0.6065306597126334
C_MASKF, C_MASKB, C_ID, C_MTF, C_MTB, C_HSEL, C_BONES, C_RESET, C_ID128 = 0, 128, 256, 320, 384, 448, 450, 578, 834
NCST = 834 + 128
EPS = 1e-6
GN_EPS = 64e-5


class Op:
    __slots__ = ("eng", "fn", "deps", "seq", "is_dma", "sem", "semval", "sig", "idx")


class Prog:
    ENGS = ("pe", "act", "dve", "pool", "sp")

    def __init__(self):
        self.ops = []
        self.last_w = {}
        self.readers = {}
        self.dma_cnt = {}
        self.pending = {}
        self.last_eng = {}
        self.last_dma = {}

    def barrier(self):
        snap = set(self.last_eng.values()) | set(self.last_dma.values())
        for e in self.ENGS:
            self.pending[e] = set(snap) | self.pending.get(e, set())

    def add(self, eng, fn, reads=(), writes=(), dma=None):
        op = Op()
        op.eng = eng
        op.fn = fn
        op.is_dma = dma is not None
        op.sig = False
        op.seq = 0
        op.idx = len(self.ops)
        deps = set()
        for r in reads:
            w = self.last_w.get(r)
            if w is not None:
                deps.add(w)
        for k in writes:
            w = self.last_w.get(k)
            if w is not None:
                deps.add(w)
            rl = self.readers.get(k)
            if rl:
                deps.update(rl)
        pend = self.pending.pop(eng, None)
        if pend:
            deps.update(pend)
        if eng == "pe" and not op.is_dma:
            deps = {d for d in deps if d.is_dma or d.eng != "pe"}
        op.deps = deps
        if op.is_dma:
            self.last_dma[dma] = op
        else:
            self.last_eng[eng] = op
        for d in deps:
            d.sig = True
        for r in reads:
            self.readers.setdefault(r, []).append(op)
        for k in writes:
            self.last_w[k] = op
            self.readers[k] = []
        if dma is not None:
            c = self.dma_cnt.get(dma, 0) + 16
            self.dma_cnt[dma] = c
            op.sem = dma
            op.semval = c
        self.ops.append(op)
        return op

    def flush(self, nc, final=False):
        if not hasattr(self, "emitted"):
            self.emitted = 0
            self.cnt = {e: 0 for e in self.ENGS}
            self.eng_sem = {e: nc.alloc_semaphore(name="s_" + e) for e in self.ENGS}
            self.dma_sem = {}
            self.waited = {e: {} for e in self.ENGS}
        ops = self.ops[self.emitted:]
        self.emitted = len(self.ops)
        lasts = set(self.last_eng.values()) | set(self.last_dma.values())
        for w in lasts:
            w.sig = True
        for op in ops:
            if op.is_dma:
                if op.sem not in self.dma_sem:
                    self.dma_sem[op.sem] = nc.alloc_semaphore(name="d_%d" % len(self.dma_sem))
                continue
            if op.sig:
                self.cnt[op.eng] += 1
                op.seq = self.cnt[op.eng]
        per_eng = {e: [] for e in self.ENGS}
        for op in ops:
            per_eng[op.eng].append(op)
        eng_sem, dma_sem = self.eng_sem, self.dma_sem

        def run(e, ename):
            waited = self.waited[ename]

            def do_waits(deps):
                need = {}
                for d in deps:
                    if d.is_dma:
                        key = ("d", d.sem)
                        sem = dma_sem[d.sem]
                        val = d.semval
                    else:
                        key = ("e", d.eng)
                        sem = eng_sem[d.eng]
                        val = d.seq
                    if need.get(key, (None, 0))[1] < val:
                        need[key] = (sem, val)
                for key, (sem, val) in need.items():
                    if waited.get(key, 0) >= val:
                        continue
                    e.wait_ge(sem, val)
                    waited[key] = val

            for op in per_eng[ename]:
                do_waits(op.deps)
                inst = op.fn(e)
                if op.is_dma:
                    inst.then_inc(dma_sem[op.sem], 16)
                elif op.sig:
                    inst.then_inc(eng_sem[ename], 1)
            if final and ename == "sp":
                do_waits(lasts)

        with nc.Block() as block:
            @block.tensor
            def _(e):
                run(e, "pe")

            @block.scalar
            def _(e):
                run(e, "act")

            @block.vector
            def _(e):
                run(e, "dve")

            @block.gpsimd
            def _(e):
                run(e, "pool")

            @block.sync
            def _(e):
                run(e, "sp")
        self.barrier()


def _fm(v):
    return np.ascontiguousarray(np.asarray(v, np.float32).reshape(-1, 128).T)


PV_NAMES = (["n1g0", "n1g1", "n2g0", "n2g1", "fg"] + ["mix%d" % m for m in range(6)]
            + ["w00", "w01", "a00", "a01", "kk", "ka", "rk", "cw0", "cw1", "cw2"])
PV_OFF = {n: i * 16 for i, n in enumerate(PV_NAMES)}
PV_OFF["adab0"] = len(PV_NAMES) * 16
PV_OFF["adab1"] = PV_OFF["adab0"] + 96
NPV = PV_OFF["adab1"] + 96


def make_cst():
    c = np.zeros((128, NCST), np.float32)
    j = (np.arange(128) % 64)[:, None]
    i = np.arange(64)[None, :]
    c[:, C_MASKF:C_MASKF + 64] = (j < i)
    c[:, C_MASKF + 64:C_MASKF + 128] = (j <= i)
    c[:, C_MASKB:C_MASKB + 64] = (j > i)
    c[:, C_MASKB + 64:C_MASKB + 128] = (j >= i)
    c[:, C_ID:C_ID + 64] = (j == i)
    c[:, C_MTF:C_MTF + 64] = (i < j)
    c[:, C_MTB:C_MTB + 64] = (i > j)
    c[0:64, C_HSEL] = 1.0
    c[64:128, C_HSEL + 1] = 1.0
    p = np.arange(128)[:, None] // 64
    q = np.arange(128)[None, :] // 64
    c[:, C_BONES:C_BONES + 128] = (p == q)
    c[:, C_RESET:C_RESET + 256] = 1.0
    c[:, C_RESET:C_RESET + 256:64] = 0.0
    c[:, C_ID128:C_ID128 + 128] = np.eye(128, dtype=np.float32)
    return c


def pack_pvec(inp):
    pv = np.zeros((128, NPV), np.float32)

    def put(name, v):
        a = _fm(v)
        pv[:, PV_OFF[name]:PV_OFF[name] + a.shape[1]] = a

    put("n1g0", inp["norm1_g"][0]); put("n1g1", inp["norm1_g"][1])
    put("n2g0", inp["norm2_g"][0]); put("n2g1", inp["norm2_g"][1])
    put("fg", inp["final_g"])
    for m in range(6):
        put("mix%d" % m, inp["rw_mix"][0, m])
    put("w00", inp["rw_w0"][0, 0]); put("w01", inp["rw_w0"][0, 1])
    put("a00", inp["rw_a0"][0, 0]); put("a01", inp["rw_a0"][0, 1])
    put("kk", inp["rw_kk"][0]); put("ka", inp["rw_ka"][0]); put("rk", inp["rw_rk"][0].reshape(-1))
    for i in range(3):
        put("cw%d" % i, inp["sc_conv"][0, i])
    put("adab0", inp["ada_b"][0]); put("adab1", inp["ada_b"][1])
    return pv


class Builder:
    def __init__(self, start_layer=0, dbg=False, stop_after=None, only_hq=None, scan_limit=None):
        self.dbg = dbg
        self.only_hq = only_hq
        self.proj01 = bool(scan_limit) and scan_limit <= 8 and only_hq is not None and dbg == 'sim'
        self.scan_limit = scan_limit
        self.dumps = []
        nc = self.nc = bass.Bass("TRN2", target_bir_lowering=False)
        P = self.P = Prog()
        di = lambda name, shape: nc.dram_tensor(name, shape, F32, kind="ExternalInput").ap()
        self.xT = di("xT", [D, T])
        self.ctxT = di("ctxT", [D, TC])
        self.cond = di("cond", [128, KC, 2])
        self.pvec = di("pvec", [128, NPV])
        self.ada_w = di("ada_w", [2, D, 6 * D])
        self.sc_win = di("sc_win", [D, 3 * D])
        self.sc_wout = di("sc_wout", [D, D])
        self.ffn_w13 = di("ffn_w13", [2, D, 2 * FF])
        self.ffn_w2 = di("ffn_w2", [2, FF, D])
        self.rw_wr = di("rw_wr", [D, D]); self.rw_wk = di("rw_wk", [D, D])
        self.rw_wv = di("rw_wv", [D, D]); self.rw_wo = di("rw_wo", [D, D])
        self.rw_w1 = di("rw_w1", [2, D, 96]); self.rw_w2 = di("rw_w2", [2, 96, D])
        self.rw_a1 = di("rw_a1", [2, D, 96]); self.rw_a2 = di("rw_a2", [2, 96, D])
        self.rw_g1 = di("rw_g1", [D, 256]); self.rw_g2 = di("rw_g2", [256, D])
        self.lnwb = di("lnwb", [2, 128, D])
        self.cst = di("cst", [128, NCST])
        self.outT = nc.dram_tensor("outT", [D, T], F32, kind="ExternalOutput").ap()
        dk = "ExternalOutput" if dbg else "Internal"
        dr = lambda name, shape, dt=F32: nc.dram_tensor(name, shape, dt, kind=dk).ap()
        self.HXD = dr("HXD", [D, TA], BF16)
        self.LBKD = [dr("LBKD%d" % d, [D, NCHK, 128]) for d in range(2)]
        self.RARD = [dr("RARD%d" % d, [D, NCHK, 128]) for d in range(2)]
        self.DCD = [dr("DCD%d" % d, [D, NCHK]) for d in range(2)]
        self.VD = dr("VD", [TA, D])
        self.GD = dr("GD", [T, D])
        self.BOND = dr("BOND", [T, NH])
        self.YD = [dr("YD%d" % d, [T, D]) for d in range(2)]
        kind = "ExternalOutput" if dbg else "Internal"
        self.XT = nc.dram_tensor("XT", [D, T], F32, kind=kind).ap()
        self.ZD = nc.dram_tensor("ZD", [D, T + 2], F32, kind="Internal").ap()
        self.GBD = nc.dram_tensor("GBD", [D, T], BF16, kind="Internal").ap()
        sb = lambda name, shape, dt: nc.alloc_sbuf_tensor(name, shape, dt).ap()
        self.SQ = [sb("sq%d" % i, [128, 512], F32R) for i in range(2)]
        self.PV = sb("pv", [128, NPV], F32)
        self.CT = sb("ct", [128, KC, 2], F32)
        self.MOD = [sb("mod%d" % i, [128, 96, 2], F32) for i in range(2)]
        self.MS = [sb("ms%d" % i, [128, 6, 16], F32) for i in range(2)]
        self.MSC = sb("msc", [128, 2, 16], F32)
        self.ONES = sb("ones", [128, 128], F32R)
        self.ONESF = sb("onesf", [128, 128], F32)
        self.ZERO = sb("zero", [128, 512], F32)
        self.CST = sb("cstt", [128, NCST], F32)
        self.OMKA = sb("omka", [128, 16], F32)
        self.BONESr = sb("bonesr", [128, 128], F32R)
        self.HSELr = sb("hselr", [128, 2], F32R)
        self.PRr = sb("prr", [128, 256], F32R)
        self.IDr = sb("idr", [128, 128], F32R)
        self.IDb = sb("idb", [128, 128], BF16)
        self.ps = [nc.alloc_psum_tensor("ps%d" % i, [128, 512], F32).ap() for i in range(8)]
        self.ws_i = 0
        self.ps_i = 0
        self.sq_i = 0
        self.tmp_i = 0
        self.stop_after = stop_after
        fin = lambda: P.flush(nc, final=True)
        with self.std_bufs("a"):
            self.prologue()
            self.adaln()
            if dbg:
                self.dump("d_ms0", self.MS[0], [128, 6, 16], "ms0")
                self.dump("d_ms1", self.MS[1], [128, 6, 16], "ms1")
                self.dump("d_msc", self.MSC, [128, 2, 16], "msc")
            if start_layer == 0:
                self.rwkv_norm()
                P.barrier()
                self.rwkv_proj(only_tiles=([int(x) for x in os.environ['PROJ_TILES'].split(',')] if os.environ.get('PROJ_TILES') else ([0, 1] if (stop_after == "proj01" or getattr(self, 'proj01', False)) else None)))
                if stop_after in ("proj", "proj01"):
                    return fin()
                P.flush(nc)
        if start_layer == 0:
            with self.scan_bufs():
                self.rwkv_scan()
                if stop_after == "scan":
                    return fin()
                P.flush(nc)
        with self.std_bufs("b"):
            if start_layer == 0:
                self.rwkv_readout()
                if stop_after == "mix0":
                    return fin()
                P.barrier()
                self.ffn(0)
                if stop_after == "l0":
                    return fin()
                P.barrier()
            else:
                self.copy_x_to_XT()
            self.layer1_mixer()
            P.barrier()
            self.ffn(1, final=True)
            fin()

    @contextlib.contextmanager
    def std_bufs(self, sfx):
        nc = self.nc
        with contextlib.ExitStack() as st:
            sb = lambda name, shape, dt: st.enter_context(nc.sbuf_tensor(name + sfx, shape, dt)).ap()
            self.WS = [sb("ws%d" % i, [128, 8192], BF16) for i in range(4)]
            self.BIGA = sb("biga", [128, 24704], BF16)
            self.XTB = sb("xtb", [128, KC, 512], F32)
            self.HB = sb("hb", [128, KC, 512], BF16)
            self.TMP = [sb("tmp%d" % i, [128, 512], F32) for i in range(3)]
            self.RS = sb("rs", [128, 512], F32)
            self.RWB = sb("rwb", [128, 10240], BF16)
            yield

    def dump(self, name, ap, shape, key, dt=F32):
        d = self.nc.dram_tensor(name, shape, dt, kind="ExternalOutput").ap()
        self.P.add("sp", lambda e: e.dma_start(out=d, in_=ap), reads=[key], writes=[name], dma=name)
        self.dumps.append(name)

    def pvc(self, name, c=None):
        o = PV_OFF[name]
        if c is None:
            return self.PV[:, o:o + 16]
        return self.PV[:, o + c:o + c + 1]

    def next_ws(self):
        i = self.ws_i
        self.ws_i = (i + 1) % 4
        return i

    def next_ps(self, lo=0, hi=4):
        i = lo + self.ps_i % (hi - lo)
        self.ps_i += 1
        return i

    def load_w(self, src3, kp, nk, ncols):
        s = self.next_ws()
        view = self.WS[s][:, 0:nk * ncols].rearrange("p (k n) -> p k n", n=ncols)
        dst = view[0:kp, :, :]
        self.P.add("pool", lambda e: e.dma_start(out=dst, in_=src3), writes=["ws%d" % s], dma="ws%d" % s)
        return s, view

    def prologue(self):
        P = self.P
        P.add("sp", lambda e: e.dma_start(out=self.PV, in_=self.pvec), writes=["pv"], dma="pv")
        P.add("sp", lambda e: e.dma_start(out=self.CT, in_=self.cond), writes=["ct"], dma="ct")
        P.add("sp", lambda e: e.dma_start(out=self.CST, in_=self.cst), writes=["cst"], dma="cst")
        P.add("dve", lambda e: e.tensor_scalar(out=self.OMKA, in0=self.pvc("ka"), scalar1=-1.0, scalar2=1.0,
                                               op0=ALU.mult, op1=ALU.add), reads=["pv"], writes=["omka"])
        P.add("dve", lambda e: e.memset(self.ONESF, 1.0), writes=["onesf"])
        P.add("dve", lambda e: e.tensor_copy(out=self.ONES, in_=self.ONESF), reads=["onesf"], writes=["ones"])
        P.add("dve", lambda e: e.memset(self.ZERO, 0.0), writes=["zero"])
        P.add("dve", lambda e: e.tensor_copy(out=self.IDr, in_=self.CST[:, C_ID128:C_ID128 + 128]), reads=["cst"], writes=["idr"])
        P.add("dve", lambda e: e.tensor_copy(out=self.IDb, in_=self.CST[:, C_ID128:C_ID128 + 128]), reads=["cst"], writes=["idb"])
        P.add("act", lambda e: e.activation(out=self.CT, in_=self.CT, func=AF.Silu), reads=["ct"], writes=["ct"])

    def adaln(self):
        P = self.P
        for i in range(2):
            psA = self.ps[6 + i]
            wv = self.ada_w[i].rearrange("(kc p) n -> p kc n", p=128)
            for nb in range(48):
                s = self.next_ws()
                view = self.WS[s].bitcast(F32).rearrange("p (k n) -> p k n", n=256)
                src = wv[:, :, nb * 256:(nb + 1) * 256]
                P.add("sp", lambda e, view=view, src=src: e.dma_start(out=view, in_=src),
                      writes=["ws%d" % s], dma="wsa%d" % s)
                for jj in range(2):
                    j = nb * 2 + jj
                    for kc in range(KC):
                        P.add("pe", lambda e, view=view, jj=jj, j=j, kc=kc, psA=psA: e.matmul(
                            psA[:, 2 * j:2 * j + 2], lhsT=view[:, kc, jj * 128:(jj + 1) * 128], rhs=self.CT[:, kc, :],
                            start=(kc == 0), stop=(kc == KC - 1)),
                            reads=["ws%d" % s, "ct"], writes=["ps%d" % (6 + i)])
            ab = self.PV[:, PV_OFF["adab%d" % i]:PV_OFF["adab%d" % i] + 96]
            P.add("dve", lambda e, i=i, psA=psA, ab=ab: e.tensor_tensor(
                out=self.MOD[i], in0=psA[:, 0:192].rearrange("p (j t) -> p j t", t=2),
                in1=ab.unsqueeze(2).to_broadcast([128, 96, 2]), op=ALU.add),
                reads=["ps%d" % (6 + i), "pv"], writes=["mod%d" % i])
            M = self.MOD[i]
            MS = self.MS[i]
            n1g = self.pvc("n1g%d" % i)
            n2g = self.pvc("n2g%d" % i)
            P.add("dve", lambda e, M=M, MS=MS, n1g=n1g: e.scalar_tensor_tensor(
                out=MS[:, 0, :], in0=M[:, 16:32, 0], scalar=1.0, in1=n1g, op0=ALU.add, op1=ALU.mult),
                reads=["mod%d" % i, "pv"], writes=["ms%d" % i])
            P.add("dve", lambda e, M=M, MS=MS: e.tensor_copy(out=MS[:, 1, :], in_=M[:, 0:16, 0]),
                  reads=["mod%d" % i], writes=["ms%d" % i])
            P.add("dve", lambda e, M=M, MS=MS: e.tensor_copy(out=MS[:, 2, :], in_=M[:, 32:48, 0]),
                  reads=["mod%d" % i], writes=["ms%d" % i])
            P.add("dve", lambda e, M=M, MS=MS, n2g=n2g: e.scalar_tensor_tensor(
                out=MS[:, 3, :], in0=M[:, 64:80, 0], scalar=1.0, in1=n2g, op0=ALU.add, op1=ALU.mult),
                reads=["mod%d" % i, "pv"], writes=["ms%d" % i])
            P.add("dve", lambda e, M=M, MS=MS: e.tensor_copy(out=MS[:, 4, :], in_=M[:, 48:64, 0]),
                  reads=["mod%d" % i], writes=["ms%d" % i])
            P.add("dve", lambda e, M=M, MS=MS: e.tensor_copy(out=MS[:, 5, :], in_=M[:, 80:96, 0]),
                  reads=["mod%d" % i], writes=["ms%d" % i])
            if i == 0:
                P.add("dve", lambda e, M=M, n1g=n1g: e.scalar_tensor_tensor(
                    out=self.MSC[:, 0, :], in0=M[:, 16:32, 1], scalar=1.0, in1=n1g, op0=ALU.add, op1=ALU.mult),
                    reads=["mod0", "pv"], writes=["msc"])
                P.add("dve", lambda e, M=M: e.tensor_copy(out=self.MSC[:, 1, :], in_=M[:, 0:16, 1]),
                      reads=["mod0"], writes=["msc"])

    def norm_mod(self, X, xkey, A, B, akey, out, okey, n, bkey=None):
        P = self.P
        psn = 6
        psN = self.ps[psn]
        for c in range(KC):
            q = self.sq_i % 2
            self.sq_i += 1
            SQ = self.SQ[q]
            P.add("act", lambda e, SQ=SQ, c=c: e.activation(out=SQ[:, :n], in_=X[:, c, :n], func=AF.Square),
                  reads=[xkey], writes=["sq%d" % q])
            P.add("pe", lambda e, SQ=SQ, c=c: e.matmul(psN[:, :n], lhsT=self.ONES, rhs=SQ[:, :n],
                                                      start=(c == 0), stop=(c == KC - 1)),
                  reads=["sq%d" % q, "ones"], writes=["ps%d" % psn])
        P.add("act", lambda e: e.activation(out=self.RS[:, :n], in_=psN[:, :n], func=AF.Sqrt, scale=1.0 / D, bias=EPS),
              reads=["ps%d" % psn], writes=["rs"])
        P.add("dve", lambda e: e.reciprocal(out=self.RS[:, :n], in_=self.RS[:, :n]), reads=["rs"], writes=["rs"])
        for c in range(KC):
            q = self.tmp_i % 3
            self.tmp_i += 1
            TM = self.TMP[q]
            P.add("dve", lambda e, TM=TM, c=c: e.scalar_tensor_tensor(
                out=TM[:, :n], in0=X[:, c, :n], scalar=A[:, c:c + 1], in1=self.RS[:, :n], op0=ALU.mult, op1=ALU.mult),
                reads=[xkey, "rs", akey], writes=["tmp%d" % q])
            if B is not None:
                P.add("act", lambda e, TM=TM, c=c: e.activation(out=out[:, c, :n], in_=TM[:, :n], func=AF.Identity,
                                                               bias=B[:, c:c + 1]),
                      reads=["tmp%d" % q, akey], writes=[okey])
            else:
                P.add("act", lambda e, TM=TM, c=c: e.copy(out=out[:, c, :n], in_=TM[:, :n]),
                      reads=["tmp%d" % q], writes=[okey])

    def copy_x_to_XT(self):
        P = self.P
        for t in range(T // 512):
            src = self.xT.rearrange("(c p) t -> p c t", p=128)[:, :, t * 512:(t + 1) * 512]
            dst = self.XT.rearrange("(c p) t -> p c t", p=128)[:, :, t * 512:(t + 1) * 512]
            P.add("sp", lambda e, src=src: e.dma_start(out=self.XTB, in_=src), writes=["xtb"], dma="xtb")
            P.add("sp", lambda e, dst=dst: e.dma_start(out=dst, in_=self.XTB), reads=["xtb"], writes=["XT%d" % t], dma="xtbo")

    def rwkv_norm(self):
        P = self.P
        HXv = self.HXD.rearrange("(c p) t -> p c t", p=128)
        tiles = [(self.ctxT, 0, TC, 0, self.MSC[:, 0, :], self.MSC[:, 1, :], "msc")]
        for t in range(T // 512):
            tiles.append((self.xT, t * 512, 512, TC + t * 512, self.MS[0][:, 0, :], self.MS[0][:, 1, :], "ms0"))
        for (src, s0, n, a0, A, B, akey) in tiles:
            sv = src.rearrange("(c p) t -> p c t", p=128)[:, :, s0:s0 + n]
            P.add("sp", lambda e, sv=sv, n=n: e.dma_start(out=self.XTB[:, :, :n], in_=sv), writes=["xtb"], dma="xtb")
            self.norm_mod(self.XTB, "xtb", A, B, akey, self.HB, "hb", n)
            P.add("sp", lambda e, a0=a0, n=n: e.dma_start(out=HXv[:, :, a0:a0 + n], in_=self.HB[:, :, :n]),
                  reads=["hb"], writes=[], dma="hxst")

    def rwkv_proj(self, only_tiles=None):
        P = self.P
        n = 256
        nch = n // CH
        HXv = self.HXD.rearrange("(c p) t -> p c t", p=128)
        HBa = self.HB[:, :, 0:n]
        HBb = self.HB[:, :, n:2 * n]
        XM = self.BIGA[:, 0:4096].rearrange("p (c t) -> p c t", t=n)
        XR = self.BIGA[:, 4096:8192].rearrange("p (c t) -> p c t", t=n)
        XK = self.BIGA[:, 8192:12288].rearrange("p (c t) -> p c t", t=n)
        VT = self.BIGA[:, 16384:24576].bitcast(F32).rearrange("p (s f) -> p s f", f=D)
        L2W = self.RWB[:, 0:8192].rearrange("p (w f) -> p w f", f=D)
        HL = self.RWB[:, 8192:9216].rearrange("p (w t) -> p w t", t=n)
        HG = self.RWB[:, 9216:9728].rearrange("p (w t) -> p w t", t=n)
        BONS = self.RWB[:, 9728:9856].bitcast(F32).rearrange("p (s h) -> p s h", h=NH)
        XQ = self.XTB.rearrange("p c t -> p (c t)")

        def xq(k, w=1):
            return XQ[:, k * n:(k + w) * n]

        Qr, Qk, Qkk, Qsq, Qsig, Qa, Qcs, Qcx, Qer, Qek, Qea, Qt10, Qt11, Qkd, Qt2, Qpr, Qinv, Qtot = range(18)
        QL = [18, 22]
        QR = [20, 24]
        QDC = 26
        cst = self.CST
        MRESET = cst[:, C_RESET:C_RESET + n]
        BONESr = self.BONESr
        HSELr = self.HSELr
        P.add("dve", lambda e: e.tensor_copy(out=BONESr, in_=cst[:, C_BONES:C_BONES + 128]), reads=["cst"], writes=["bones"])
        P.add("dve", lambda e: e.tensor_copy(out=HSELr, in_=cst[:, C_HSEL:C_HSEL + 2]), reads=["cst"], writes=["hsel"])
        for w, src in enumerate([self.rw_w2[0], self.rw_w2[1], self.rw_a2[0], self.rw_a2[1]]):
            P.add("pool", lambda e, w=w, src=src: e.dma_start(out=L2W[0:96, w, :], in_=src), writes=["l2w"], dma="l2w%d" % w)
        wr = self.rw_wr.rearrange("(kc p) n -> p kc n", p=128)
        wk = self.rw_wk.rearrange("(kc p) n -> p kc n", p=128)
        wv = self.rw_wv.rearrange("(kc p) n -> p kc n", p=128)
        g1 = self.rw_g1.rearrange("(kc p) n -> p kc n", p=128)
        g2 = self.rw_g2.rearrange("(kc p) n -> p kc n", p=128)
        w1 = [self.rw_w1[d].rearrange("(kc p) n -> p kc n", p=128) for d in range(2)]
        a1 = [self.rw_a1[d].rearrange("(kc p) n -> p kc n", p=128) for d in range(2)]
        psB = self.ps[7]

        def mix(m, dst, dkey):
            for c in range(KC):
                P.add("dve", lambda e, c=c: e.scalar_tensor_tensor(
                    out=dst[:, c, :], in0=HBb[:, c, :], scalar=self.pvc("mix%d" % m, c), in1=HBa[:, c, :],
                    op0=ALU.mult, op1=ALU.add), reads=["hb", "pv"], writes=[dkey])

        def mm_fm(view, s, col0, M, rhs3, rkey, pi, kp=128, nk=KC):
            for kc in range(nk):
                P.add("pe", lambda e, kc=kc: e.matmul(self.ps[pi][0:M, :n], lhsT=view[0:kp, kc, col0:col0 + M],
                                                     rhs=rhs3[0:kp, kc, :], start=(kc == 0), stop=(kc == nk - 1)),
                      reads=["ws%d" % s, rkey], writes=["ps%d" % pi])

        for ti in range(1 + T // n):
            if only_tiles is not None and ti not in only_tiles:
                continue
            is_ctx = ti == 0
            a0 = 0 if is_ctx else TC + (ti - 1) * n
            t0 = 0 if is_ctx else (ti - 1) * n
            ch0 = a0 // CH
            P.add("sp", lambda e, a0=a0: e.dma_start(out=HBa, in_=HXv[:, :, a0:a0 + n]), writes=["hb"], dma="hxld")
            HB4 = HBb.rearrange("p c (r w) -> p c r w", w=CH)
            if is_ctx:
                P.add("sp", lambda e: e.dma_start(out=HBb[:, 0:8, 1:n], in_=HXv[:, 0:8, 0:n - 1]), writes=["hb"], dma="shld0")
                P.add("sp", lambda e: e.dma_start(out=HBb[:, 8:16, 0:n - 1], in_=HXv[:, 8:16, 1:n]), writes=["hb"], dma="shld1")
                P.add("dve", lambda e: e.memset(HBb[:, 0:8, 0:1], 0.0), writes=["hb"])
                P.add("dve", lambda e: e.memset(HBb[:, 8:16, n - 1:n], 0.0), writes=["hb"])
            else:
                P.add("sp", lambda e, a0=a0: e.dma_start(out=HBb[:, 0:4, 1:n], in_=HXv[:, 0:4, a0:a0 + n - 1]), writes=["hb"], dma="shld0")
                P.add("sp", lambda e, a0=a0: e.dma_start(out=HBb[:, 4:8, 0:n - 1], in_=HXv[:, 4:8, a0 + 1:a0 + n]), writes=["hb"], dma="shld1")
                if t0 == 0:
                    P.add("sp", lambda e, a0=a0: e.dma_start(out=HBb[:, 8:12, CH:n], in_=HXv[:, 8:12, a0:a0 + n - CH]), writes=["hb"], dma="shld2")
                    P.add("dve", lambda e: e.memset(HBb[:, 8:12, 0:CH], 0.0), writes=["hb"])
                else:
                    P.add("sp", lambda e, a0=a0: e.dma_start(out=HBb[:, 8:12, :], in_=HXv[:, 8:12, a0 - CH:a0 + n - CH]), writes=["hb"], dma="shld2")
                if t0 == T - n:
                    P.add("sp", lambda e, a0=a0: e.dma_start(out=HBb[:, 12:16, 0:n - CH], in_=HXv[:, 12:16, a0 + CH:a0 + n]), writes=["hb"], dma="shld3")
                    P.add("dve", lambda e: e.memset(HBb[:, 12:16, n - CH:n], 0.0), writes=["hb"])
                else:
                    P.add("sp", lambda e, a0=a0: e.dma_start(out=HBb[:, 12:16, :], in_=HXv[:, 12:16, a0 + CH:a0 + n + CH]), writes=["hb"], dma="shld3")
                P.add("dve", lambda e: e.memset(HB4[:, 0:4, :, 0:1], 0.0), writes=["hb"])
                P.add("dve", lambda e: e.memset(HB4[:, 4:8, :, CH - 1:CH], 0.0), writes=["hb"])
            P.add("dve", lambda e: e.tensor_tensor(out=HBb, in0=HBb, in1=HBa, op=ALU.subtract), reads=["hb"], writes=["hb"])
            for m, wsrc, w_off, func in ((1, w1, 0, AF.Tanh), (4, a1, 2, AF.Identity)):
                mix(m, XM, "bg0")
                for d in range(2):
                    s, view = self.load_w(wsrc[d], 128, KC, 96)
                    pi = self.next_ps(0, 6)
                    mm_fm(view, s, 0, 96, XM, "bg0", pi)
                    P.add("act", lambda e, pi=pi, d=d, w_off=w_off, func=func: e.activation(
                        out=HL[0:96, w_off + d, :], in_=self.ps[pi][0:96, :n], func=func),
                        reads=["ps%d" % pi], writes=["hl"])
            if not is_ctx:
                mix(5, XM, "bg0")
                s, view = self.load_w(g1, 128, KC, 256)
                for cc in range(2):
                    pi = self.next_ps(0, 6)
                    mm_fm(view, s, cc * 128, 128, XM, "bg0", pi)
                    P.add("act", lambda e, pi=pi, cc=cc: e.activation(out=HG[:, cc, :], in_=self.ps[pi][:, :n], func=AF.Sigmoid),
                          reads=["ps%d" % pi], writes=["hg"])
                for nb in range(4):
                    s, view = self.load_w(g2[:, :, nb * 512:(nb + 1) * 512], 128, 2, 512)
                    for su in range(n // 128):
                        pi = self.next_ps(0, 6)
                        for kc in range(2):
                            P.add("pe", lambda e, kc=kc, su=su, view=view, pi=pi: e.matmul(
                                self.ps[pi], lhsT=HG[:, kc, su * 128:(su + 1) * 128], rhs=view[:, kc, :],
                                start=(kc == 0), stop=(kc == 1)), reads=["ws%d" % s, "hg"], writes=["ps%d" % pi])
                        P.add("act", lambda e, pi=pi, su=su, nb=nb: e.copy(out=VT[:, su, nb * 512:(nb + 1) * 512], in_=self.ps[pi]),
                              reads=["ps%d" % pi], writes=["bg1"])
                P.add("sp", lambda e, t0=t0: e.dma_start(
                    out=self.GD[t0:t0 + n, :].rearrange("(s p) f -> p s f", p=128), in_=VT),
                    reads=["bg1"], writes=[], dma="gst")
            mix(3, XM, "bg0")
            for nb in range(4):
                s, view = self.load_w(wv[:, :, nb * 512:(nb + 1) * 512], 128, KC, 512)
                for su in range(n // 128):
                    pi = self.next_ps(0, 6)
                    for kc in range(KC):
                        P.add("pe", lambda e, kc=kc, su=su, view=view, pi=pi: e.matmul(
                            self.ps[pi], lhsT=XM[:, kc, su * 128:(su + 1) * 128], rhs=view[:, kc, :],
                            start=(kc == 0), stop=(kc == KC - 1)), reads=["ws%d" % s, "bg0"], writes=["ps%d" % pi])
                    P.add("act", lambda e, pi=pi, su=su, nb=nb: e.copy(out=VT[:, su, nb * 512:(nb + 1) * 512], in_=self.ps[pi]),
                          reads=["ps%d" % pi], writes=["bg1"])
            P.add("sp", lambda e, a0=a0: e.dma_start(
                out=self.VD[a0:a0 + n, :].rearrange("(s p) f -> p s f", p=128), in_=VT),
                reads=["bg1"], writes=[], dma="vst")
            mix(0, XR, "bgr")
            mix(2, XK, "bgk")
            Sr, Sk, Skk, Sinv = 0, 1, 2, 3
            SD = lambda d, k: 4 + 10 * d + k
            Stot = [24, 25]

            def slot(k, w=1):
                return XQ[:, k * n:(k + w) * n]

            LBt = [slot(26, 2).rearrange("p (c w) -> p c w", w=128), slot(30, 2).rearrange("p (c w) -> p c w", w=128)]
            RAt = [slot(28, 2).rearrange("p (c w) -> p c w", w=128), self.TMP[0].rearrange("p (c w) -> p c w", w=128)]
            kLB = ["xq26", "xq30"]
            kRA = ["xq28", "tmp0"]
            kq = lambda k: "xq%d" % k
            v3 = lambda ap: ap.rearrange("p (c t) -> p c t", t=CH)
            for j in range(KC):
                sr, vr = self.load_w(wr[:, :, j * 128:(j + 1) * 128], 128, KC, 128)
                sk, vk = self.load_w(wk[:, :, j * 128:(j + 1) * 128], 128, KC, 128)
                p_r = self.next_ps(0, 6)
                mm_fm(vr, sr, 0, 128, XR, "bgr", p_r)
                p_k = self.next_ps(0, 6)
                mm_fm(vk, sk, 0, 128, XK, "bgk", p_k)
                p_z, p_a = [], []
                for d in range(2):
                    pz = self.next_ps(0, 6)
                    P.add("pe", lambda e, d=d, j=j, pz=pz: e.matmul(self.ps[pz][:, :n], lhsT=L2W[0:96, d, j * 128:(j + 1) * 128],
                                                                   rhs=HL[0:96, d, :], start=True, stop=True),
                          reads=["l2w", "hl"], writes=["ps%d" % pz])
                    pa = self.next_ps(0, 6)
                    P.add("pe", lambda e, d=d, j=j, pa=pa: e.matmul(self.ps[pa][:, :n], lhsT=L2W[0:96, 2 + d, j * 128:(j + 1) * 128],
                                                                   rhs=HL[0:96, 2 + d, :], start=True, stop=True),
                          reads=["l2w", "hl"], writes=["ps%d" % pa])
                    p_z.append(pz)
                    p_a.append(pa)
                P.add("act", lambda e, p_k=p_k: e.copy(out=slot(Sk), in_=self.ps[p_k][:, :n]), reads=["ps%d" % p_k], writes=[kq(Sk)])
                P.add("act", lambda e, p_r=p_r: e.copy(out=slot(Sr), in_=self.ps[p_r][:, :n]), reads=["ps%d" % p_r], writes=[kq(Sr)])
                for d in range(2):
                    P.add("act", lambda e, d=d, j=j, pz=p_z[d]: e.activation(out=slot(SD(d, 0)), in_=self.ps[pz][:, :n], func=AF.Sigmoid,
                                                                            bias=self.pvc("w0%d" % d, j)),
                          reads=["ps%d" % p_z[d], "pv"], writes=[kq(SD(d, 0))])
                for d in range(2):
                    P.add("act", lambda e, d=d, j=j, pa=p_a[d]: e.activation(out=slot(SD(d, 1)), in_=self.ps[pa][:, :n], func=AF.Sigmoid,
                                                                            bias=self.pvc("a0%d" % d, j)),
                          reads=["ps%d" % p_a[d], "pv"], writes=[kq(SD(d, 1))])
                P.add("dve", lambda e, j=j: e.tensor_scalar(out=slot(Skk), in0=slot(Sk), scalar1=self.pvc("kk", j), scalar2=None, op0=ALU.mult),
                      reads=[kq(Sk), "pv"], writes=[kq(Skk)])
                for d in range(2):
                    P.add("dve", lambda e, d=d: e.tensor_tensor_scan(out=slot(SD(d, 2)), data0=MRESET, data1=slot(SD(d, 0)), initial=0.0,
                                                                    op0=ALU.mult, op1=ALU.add),
                          reads=["cst", kq(SD(d, 0))], writes=[kq(SD(d, 2))])
                q = self.sq_i % 2
                self.sq_i += 1
                SQ = self.SQ[q]
                P.add("act", lambda e, SQ=SQ: e.activation(out=SQ[:, :n], in_=slot(Skk), func=AF.Square), reads=[kq(Skk)], writes=["sq%d" % q])
                P.add("pe", lambda e, SQ=SQ: e.matmul(self.ps[6][:, :n], lhsT=BONESr, rhs=SQ[:, :n], start=True, stop=True),
                      reads=["sq%d" % q, "bones"], writes=["ps6"])
                for d in range(2):
                    P.add("act", lambda e, d=d: e.activation(out=slot(Stot[d])[:, 0:nch], in_=v3(slot(SD(d, 2)))[:, :, CH - 1], func=AF.Exp, scale=-LWS),
                          reads=[kq(SD(d, 2))], writes=[kq(Stot[d])])
                    P.add("sp", lambda e, d=d, j=j, ch0=ch0: e.dma_start(out=self.DCD[d][j * 128:(j + 1) * 128, ch0:ch0 + nch],
                                                                       in_=slot(Stot[d])[:, 0:nch]),
                          reads=[kq(Stot[d])], writes=[], dma="dcst%d" % d)
                P.add("dve", lambda e: e.tensor_tensor(
                    out=v3(slot(SD(1, 3))), in0=v3(slot(SD(1, 0))),
                    in1=v3(slot(SD(1, 2)))[:, :, CH - 1:CH].to_broadcast([128, nch, CH]), op=ALU.add),
                    reads=[kq(SD(1, 2)), kq(SD(1, 0))], writes=[kq(SD(1, 3))])
                P.add("dve", lambda e: e.tensor_tensor(out=slot(SD(1, 2)), in0=slot(SD(1, 3)), in1=slot(SD(1, 2)), op=ALU.subtract),
                      reads=[kq(SD(1, 2)), kq(SD(1, 3)), kq(Stot[1])], writes=[kq(SD(1, 2))])
                for d in range(2):
                    P.add("dve", lambda e, d=d: e.tensor_tensor(out=slot(SD(d, 3)), in0=slot(SD(d, 2)), in1=slot(SD(d, 0)), op=ALU.subtract),
                          reads=[kq(SD(d, 2)), kq(SD(d, 0))], writes=[kq(SD(d, 3))])
                P.add("act", lambda e: e.activation(out=slot(Sinv), in_=self.ps[6][:, :n], func=AF.Sqrt), reads=["ps6"], writes=[kq(Sinv)])
                for d in range(2):
                    P.add("act", lambda e, d=d: e.activation(out=slot(SD(d, 4)), in_=slot(SD(d, 2)), func=AF.Exp, scale=-LWS),
                          reads=[kq(SD(d, 2))], writes=[kq(SD(d, 4))])
                    P.add("act", lambda e, d=d: e.activation(out=slot(SD(d, 5)), in_=slot(SD(d, 2)), func=AF.Exp, scale=LWS),
                          reads=[kq(SD(d, 2))], writes=[kq(SD(d, 5))])
                    P.add("act", lambda e, d=d: e.activation(out=slot(SD(d, 6)), in_=slot(SD(d, 3)), func=AF.Exp, scale=-LWS),
                          reads=[kq(SD(d, 3))], writes=[kq(SD(d, 6))])
                P.add("dve", lambda e: e.tensor_scalar(out=slot(Sinv), in0=slot(Sinv), scalar1=1e-12, scalar2=None, op0=ALU.max),
                      reads=[kq(Sinv)], writes=[kq(Sinv)])
                P.add("dve", lambda e: e.reciprocal(out=slot(Sinv), in_=slot(Sinv)), reads=[kq(Sinv)], writes=[kq(Sinv)])
                P.add("dve", lambda e: e.tensor_tensor(out=slot(Skk), in0=slot(Skk), in1=slot(Sinv), op=ALU.mult),
                      reads=[kq(Skk), kq(Sinv)], writes=[kq(Skk)])
                for d in range(2):
                    LB, RA = LBt[d], RAt[d]
                    P.add("dve", lambda e, d=d, j=j: e.tensor_scalar(out=slot(SD(d, 9)), in0=slot(SD(d, 1)), scalar1=self.pvc("ka", j),
                                                                    scalar2=self.OMKA[:, j:j + 1], op0=ALU.mult, op1=ALU.add),
                          reads=[kq(SD(d, 1)), "pv", "omka"], writes=[kq(SD(d, 9))])
                    P.add("dve", lambda e, d=d: e.tensor_tensor(out=slot(SD(d, 7)), in0=slot(Sk), in1=slot(SD(d, 9)), op=ALU.mult),
                          reads=[kq(Sk), kq(SD(d, 9))], writes=[kq(SD(d, 7))])
                    P.add("dve", lambda e, d=d, RA=RA: e.tensor_tensor(out=RA[:, :, CH:2 * CH], in0=v3(slot(Sr)), in1=v3(slot(SD(d, 4))), op=ALU.mult),
                          reads=[kq(Sr), kq(SD(d, 4))], writes=[kRA[d]])
                    P.add("dve", lambda e, d=d, LB=LB: e.tensor_tensor(out=LB[:, :, CH:2 * CH], in0=v3(slot(SD(d, 7))), in1=v3(slot(SD(d, 5))), op=ALU.mult),
                          reads=[kq(SD(d, 7)), kq(SD(d, 5))], writes=[kLB[d]])
                    P.add("dve", lambda e, d=d, RA=RA: e.scalar_tensor_tensor(out=RA[:, :, 0:CH], in0=v3(slot(Skk)), scalar=-1.0, in1=v3(slot(SD(d, 6))),
                                                                             op0=ALU.mult, op1=ALU.mult),
                          reads=[kq(Skk), kq(SD(d, 6))], writes=[kRA[d]])
                    P.add("dve", lambda e, d=d: e.tensor_tensor(out=slot(SD(d, 8)), in0=slot(Skk), in1=slot(SD(d, 1)), op=ALU.mult),
                          reads=[kq(Skk), kq(SD(d, 1))], writes=[kq(SD(d, 8))])
                    P.add("dve", lambda e, d=d, LB=LB: e.tensor_tensor(out=LB[:, :, 0:CH], in0=v3(slot(SD(d, 8))), in1=v3(slot(SD(d, 5))), op=ALU.mult),
                          reads=[kq(SD(d, 8)), kq(SD(d, 5))], writes=[kLB[d]])
                    P.add("sp", lambda e, d=d, j=j, ch0=ch0, LB=LB: e.dma_start(out=self.LBKD[d][j * 128:(j + 1) * 128, ch0:ch0 + nch, :], in_=LB),
                          reads=[kLB[d]], writes=[], dma="lbst%d" % d)
                    P.add("sp", lambda e, d=d, j=j, ch0=ch0, RA=RA: e.dma_start(out=self.RARD[d][j * 128:(j + 1) * 128, ch0:ch0 + nch, :], in_=RA),
                          reads=[kRA[d]], writes=[], dma="rast%d" % d)
                if not is_ctx:
                    P.add("dve", lambda e: e.tensor_tensor(out=slot(SD(0, 9)), in0=slot(SD(0, 9)), in1=slot(SD(1, 9)), op=ALU.add),
                          reads=[kq(SD(0, 9)), kq(SD(1, 9))], writes=[kq(SD(0, 9))])
                    P.add("dve", lambda e: e.tensor_tensor(out=slot(SD(0, 9)), in0=slot(SD(0, 9)), in1=slot(Sk), op=ALU.mult),
                          reads=[kq(SD(0, 9)), kq(Sk)], writes=[kq(SD(0, 9))])
                    PR = self.PRr
                    P.add("dve", lambda e, j=j, PR=PR: e.scalar_tensor_tensor(out=PR, in0=slot(SD(0, 9)), scalar=self.pvc("rk", j), in1=slot(Sr),
                                                                          op0=ALU.mult, op1=ALU.mult),
                          reads=[kq(SD(0, 9)), kq(Sr), "pv"], writes=["prr"])
                    for su in range(n // 128):
                        P.add("pe", lambda e, su=su, j=j, PR=PR: e.matmul(psB[:, su * NH + 2 * j:su * NH + 2 * j + 2],
                                                                      lhsT=PR[:, su * 128:(su + 1) * 128], rhs=HSELr, start=True, stop=True),
                              reads=["prr", "hsel"], writes=["ps7"])
            if not is_ctx:
                P.add("act", lambda e: e.copy(out=BONS, in_=psB[:, 0:2 * NH].rearrange("p (s h) -> p s h", h=NH)), reads=["ps7"], writes=["bons"])
                P.add("sp", lambda e, t0=t0: e.dma_start(out=self.BOND[t0:t0 + n, :].rearrange("(s p) h -> p s h", p=128), in_=BONS),
                      reads=["bons"], writes=[], dma="bonst")

    @contextlib.contextmanager
    def scan_bufs(self):
        nc = self.nc
        HQ = 8
        with contextlib.ExitStack() as st:
            sb = lambda name, shape, dt: st.enter_context(nc.sbuf_tensor(name, shape, dt)).ap()
            self.SC = []
            for q in range(2):
                t = {}
                t["lbkraw"] = [sb("lbkraw%d_%d" % (q, b), [64, HQ, 128], F32) for b in range(2)]
                t["rarraw"] = [sb("rarraw%d_%d" % (q, b), [64, HQ, 128], F32) for b in range(2)]
                t["uvraw"] = [sb("uvraw%d_%d" % (q, b), [128, HQ, 64], F32) for b in range(2)]
                t["dc"] = sb("dc%d" % q, [64, HQ, NCHK], F32)
                t["yt"] = [sb("yt%d_%d" % (q, b), [64, HQ, 64], F32) for b in range(2)]
                t["lbkr"] = sb("lbkr%d" % q, [64, HQ + 1, 128], F32R)
                t["rarr"] = sb("rarr%d" % q, [64, HQ + 1, 128], F32R)
                t["mb"] = sb("mb%d" % q, [128, HQ + 1, 128], F32R)
                t["nt"] = [sb("nt%d_%d" % (q, b), [64, HQ + 1, 64], F32R) for b in range(2)]
                t["nn"] = [sb("nn%d_%d" % (q, b), [64, HQ + 1, 64], F32R) for b in range(2)]
                t["zz"] = [sb("zz%d_%d" % (q, b), [64, HQ + 1, 64], F32R) for b in range(2)]
                t["xs"] = sb("xs%d" % q, [64, HQ, 64], F32R)
                t["ss"] = sb("ss%d" % q, [64, HQ, 64], F32R)
                t["uv"] = sb("uv%d" % q, [128, HQ, 64], F32R)
                t["uv0"] = sb("uv0%d" % q, [128, HQ, 64], F32R)
                t["lbkt"] = sb("lbkt%d" % q, [128, HQ + 1, 64], F32R)
                self.SC.append(t)
            yield

    def rwkv_scan(self, only_hq=None):
        only_hq = getattr(self, 'only_hq', None)
        for hq in range(4):
            if only_hq is not None and hq not in only_hq:
                continue
            gens = [self.scan_stream(q, hq, q) for q in range(2)]
            alive = list(gens)
            nsteps = 0
            maxsteps = int(os.environ.get("SCAN_STEPS", "0")) or None
            while alive:
                if maxsteps is not None and nsteps >= maxsteps:
                    break
                nsteps += 1
                for g in list(alive):
                    try:
                        next(g)
                    except StopIteration:
                        alive.remove(g)

    def scan_stream(self, q, hq, d):
        P = self.P
        HQ = 8
        tl = self.SC[q]
        K = lambda name: "s%d_%s" % (q, name)
        cst = self.CST
        MASK = cst[:, (C_MASKF if d == 0 else C_MASKB):(C_MASKF if d == 0 else C_MASKB) + 128]
        MT = cst[0:64, (C_MTF if d == 0 else C_MTB):(C_MTF if d == 0 else C_MTB) + 64]
        ID64 = cst[0:64, C_ID:C_ID + 64]
        IDr = self.IDr[0:64, :]
        LBv = self.LBKD[d].rearrange("(h p) c w -> p h c w", p=64)
        RAv = self.RARD[d].rearrange("(h p) c w -> p h c w", p=64)
        DCv = self.DCD[d].rearrange("(h p) c -> p h c", p=64)
        hs = slice(hq * HQ, (hq + 1) * HQ)
        order = list(range(NCHK)) if d == 0 else [3, 2, 1, 0] + list(range(NCHK - 1, 3, -1))
        if getattr(self, "scan_limit", None):
            order = order[:self.scan_limit]
        banks = [4 * q + b for b in range(4)]
        bstate = [0]

        def bank():
            b = banks[bstate[0] % 4]
            bstate[0] += 1
            return b

        f32 = lambda ap: ap.bitcast(F32)

        def ext(tile, hl, c0):
            w = tile.shape[2]
            return tile.rearrange("p h w -> p (h w)")[:, hl * w + c0:hl * w + c0 + 128]
        if hq == 0 or self.only_hq is not None:
            pads = [("lbkr", tl["lbkr"]), ("rarr", tl["rarr"]), ("mb", tl["mb"]), ("lbkt", tl["lbkt"])]
            pads += [("nt%d" % b, tl["nt"][b]) for b in range(2)] + [("nn%d" % b, tl["nn"][b]) for b in range(2)]
            pads += [("zz%d" % b, tl["zz"][b]) for b in range(2)]
            for nm, tile in pads:
                np_, w = tile.shape[0], tile.shape[2]
                P.add("dve", lambda e, tile=tile, np_=np_, w=w: e.tensor_copy(out=tile[:, HQ, :], in_=self.ZERO[0:np_, 0:w]),
                      reads=["zero"], writes=[K(nm)])
        if hq == 0 or self.only_hq is not None:
            P.add("dve", lambda e: e.tensor_copy(out=tl["uv0"][0:64, :, :], in_=self.ZERO[0:64, :].rearrange("p (h v) -> p h v", v=64)),
                  reads=["zero"], writes=[K("uv0")])
        P.add("dve", lambda e: e.tensor_copy(out=tl["ss"], in_=self.ZERO[0:64, :].rearrange("p (h v) -> p h v", v=64)),
              reads=["zero"], writes=[K("ss")])
        P.add("sp", lambda e: e.dma_start(out=tl["dc"], in_=DCv[:, hs, :]), writes=[K("dc")], dma=K("dc"))

        def loads(idx):
            ch = order[idx]
            b = idx % 2
            P.add("sp", lambda e: e.dma_start(out=tl["lbkraw"][b], in_=LBv[:, hs, ch, :]), writes=[K("lbkraw%d" % b)], dma=K("lbk%d" % b))
            P.add("sp", lambda e: e.dma_start(out=tl["rarraw"][b], in_=RAv[:, hs, ch, :]), writes=[K("rarraw%d" % b)], dma=K("rar%d" % b))
            P.add("sp", lambda e: e.dma_start(out=tl["uvraw"][b][64:128, :, :],
                                              in_=self.VD[ch * CH:(ch + 1) * CH, hq * 512:(hq + 1) * 512].rearrange("t (h v) -> t h v", v=64)),
                  writes=[K("uvraw%d" % b)], dma=K("uvr%d" % b))

        loads(0)
        for idx, ch in enumerate(order):
            b = idx % 2
            if idx + 1 < len(order):
                loads(idx + 1)
            lbkraw, rarraw, uvraw = tl["lbkraw"][b], tl["rarraw"][b], tl["uvraw"][b]
            lbkr, rarr, mb, uv, lbkt, xs, ss = tl["lbkr"], tl["rarr"], tl["mb"], tl["uv"], tl["lbkt"], tl["xs"], tl["ss"]
            P.add("act", lambda e, lbkraw=lbkraw: e.copy(out=lbkr[:, 0:HQ, :], in_=lbkraw), reads=[K("lbkraw%d" % b)], writes=[K("lbkr")])
            P.add("dve", lambda e, rarraw=rarraw: e.tensor_copy(out=rarr[:, 0:HQ, :], in_=rarraw), reads=[K("rarraw%d" % b)], writes=[K("rarr")])
            P.add("act", lambda e, uvraw=uvraw: e.copy(out=uv[64:128, :, :], in_=uvraw[64:128, :, :]), reads=[K("uvraw%d" % b)], writes=[K("uvv")])
            P.add("dve", lambda e, uvraw=uvraw: e.tensor_copy(out=tl["uv0"][64:128, :, :], in_=uvraw[64:128, :, :]), reads=[K("uvraw%d" % b)], writes=[K("uv0")])
            pm = [bank(), bank()]
            for hl in range(HQ):
                P.add("pe", lambda e, hl=hl, pm=pm: e.matmul(self.ps[pm[hl // 4]][:, (hl % 4) * 128:(hl % 4 + 1) * 128],
                                                            lhsT=lbkr[:, hl, :], rhs=rarr[:, hl, :], start=True, stop=True),
                      reads=[K("lbkr"), K("rarr")], writes=["ps%d" % pm[hl // 4]])
            for g in range(2):
                P.add("dve", lambda e, g=g, pm=pm: e.tensor_tensor(
                    out=mb[:, 4 * g:4 * g + 4, :], in0=self.ps[pm[g]].rearrange("p (h w) -> p h w", w=128),
                    in1=MASK.unsqueeze(1).to_broadcast([128, 4, 128]), op=ALU.mult),
                    reads=["ps%d" % pm[g], "cst"], writes=[K("mb")])
            pn = bank()
            for hl in range(HQ):
                P.add("pe", lambda e, hl=hl, pn=pn: e.matmul(self.ps[pn][:, hl * 64:(hl + 1) * 64],
                                                            lhsT=rarr[:, hl, :], rhs=lbkr[:, hl, 0:64], start=True, stop=True),
                      reads=[K("lbkr"), K("rarr")], writes=["ps%d" % pn])
            P.add("dve", lambda e, pn=pn: e.tensor_tensor(
                out=tl["nt"][0][:, 0:HQ, :], in0=self.ps[pn][0:64, :].rearrange("p (h w) -> p h w", w=64),
                in1=MT.unsqueeze(1).to_broadcast([64, HQ, 64]), op=ALU.mult),
                reads=["ps%d" % pn, "cst"], writes=[K("nt0")])
            pt = bank()
            for hl in range(HQ):
                P.add("pe", lambda e, hl=hl, pt=pt, lbkraw=lbkraw: e.transpose(self.ps[pt][:, hl * 64:(hl + 1) * 64], lbkraw[:, hl, :], ID64),
                      reads=[K("lbkraw%d" % b), "cst"], writes=["ps%d" % pt])
            P.add("act", lambda e, pt=pt: e.copy(out=lbkt[:, 0:HQ, :], in_=self.ps[pt].rearrange("p (h w) -> p h w", w=64)),
                  reads=["ps%d" % pt], writes=[K("lbkt")])
            P.add("dve", lambda e: e.tensor_tensor(out=tl["zz"][0][:, 0:HQ, :], in0=f32(mb[0:64, 0:HQ, 0:64]),
                                                   in1=ID64.unsqueeze(1).to_broadcast([64, HQ, 64]), op=ALU.add),
                  reads=[K("mb"), "cst"], writes=[K("zz0")])
            yield
            for l in range(1, 7):
                evs = []
                if l <= 5:
                    ntp, nto = tl["nt"][(l - 1) % 2], tl["nt"][l % 2]
                    kntp, knto = K("nt%d" % ((l - 1) % 2)), K("nt%d" % (l % 2))
                    if l == 1:
                        nnp_fn = lambda hl: mb[0:64, hl, 0:64]
                        nnl_fn = lambda hl: mb[0:64, hl, :]
                        knnp = K("mb")
                    else:
                        nnp_t = tl["nn"][(l - 1) % 2]
                        nnp_fn = lambda hl, nnp_t=nnp_t: nnp_t[:, hl, :]
                        nnl_fn = lambda hl, nnp_t=nnp_t: ext(nnp_t, hl, 0)
                        knnp = K("nn%d" % ((l - 1) % 2))
                    if l <= 4:
                        pa = bank()
                        for hl in range(HQ):
                            P.add("pe", lambda e, hl=hl, pa=pa, ntp=ntp, nnp_fn=nnp_fn: e.matmul(
                                self.ps[pa][:, hl * 64:(hl + 1) * 64], lhsT=ext(ntp, hl, 0), rhs=nnp_fn(hl), start=True, stop=True),
                                reads=[kntp, knnp], writes=["ps%d" % pa])
                        evs.append(("act", pa, tl["nn"][l % 2], K("nn%d" % (l % 2))))
                    pb = bank()
                    for hl in range(HQ):
                        P.add("pe", lambda e, hl=hl, pb=pb, ntp=ntp, nnl_fn=nnl_fn: e.matmul(
                            self.ps[pb][:, hl * 64:(hl + 1) * 64], lhsT=nnl_fn(hl), rhs=ntp[:, hl, :], start=True, stop=True),
                            reads=[kntp, knnp], writes=["ps%d" % pb])
                    evs.append(("act", pb, nto, knto))
                if l >= 2:
                    m = l - 1
                    ntm, kntm = tl["nt"][m % 2], K("nt%d" % (m % 2))
                    zp, kzp = tl["zz"][(m - 1) % 2], K("zz%d" % ((m - 1) % 2))
                    pz = bank()
                    for hl in range(HQ):
                        P.add("pe", lambda e, hl=hl, pz=pz, ntm=ntm, zp=zp: e.matmul(
                            self.ps[pz][:, hl * 64:(hl + 1) * 64], lhsT=ext(ntm, hl, 0), rhs=zp[:, hl, :], start=True, stop=True),
                            reads=[kntm, kzp], writes=["ps%d" % pz])
                    evs.append(("zadd", pz, tl["zz"][m % 2], K("zz%d" % (m % 2)), zp, kzp))
                for ev in evs:
                    (eng, pi, dst, dkey) = ev[:4]
                    src = self.ps[pi][0:64, :].rearrange("p (h w) -> p h w", w=64)
                    if eng == "zadd":
                        zp_, kzp_ = ev[4], ev[5]
                        P.add("dve", lambda e, src=src, dst=dst, zp_=zp_: e.tensor_tensor(
                            out=dst[:, 0:HQ, :], in0=src, in1=f32(zp_[:, 0:HQ, :]), op=ALU.add),
                            reads=["ps%d" % pi, kzp_], writes=[dkey])
                    elif eng == "act":
                        P.add("act", lambda e, src=src, dst=dst: e.copy(out=dst[:, 0:HQ, :], in_=src), reads=["ps%d" % pi], writes=[dkey])
                    else:
                        P.add("dve", lambda e, src=src, dst=dst: e.tensor_copy(out=dst[:, 0:HQ, :], in_=src), reads=["ps%d" % pi], writes=[dkey])
                yield
            TT = tl["zz"][5 % 2]
            kT = K("zz%d" % (5 % 2))
            px = bank()
            for hl in range(HQ):
                P.add("pe", lambda e, hl=hl, px=px: e.matmul(self.ps[px][:, hl * 64:(hl + 1) * 64], lhsT=rarr[:, hl, :], rhs=ss[:, hl, :],
                                                            start=True, stop=False),
                      reads=[K("rarr"), K("ss")], writes=["ps%d" % px])
                P.add("pe", lambda e, hl=hl, px=px: e.matmul(self.ps[px][:, hl * 64:(hl + 1) * 64], lhsT=mb[:, hl, :], rhs=tl["uv0"][:, hl, :],
                                                            start=False, stop=True),
                      reads=[K("mb"), K("uv0")], writes=["ps%d" % px])
            P.add("act", lambda e, px=px: e.copy(out=xs, in_=self.ps[px][0:64, :].rearrange("p (h w) -> p h w", w=64)),
                  reads=["ps%d" % px], writes=[K("xs")])
            yield
            pu = bank()
            for hl in range(HQ):
                P.add("pe", lambda e, hl=hl, pu=pu, TT=TT: e.matmul(self.ps[pu][:, hl * 64:(hl + 1) * 64], lhsT=ext(TT, hl, 0), rhs=xs[:, hl, :],
                                                                   start=True, stop=True),
                      reads=[kT, K("xs")], writes=["ps%d" % pu])
            P.add("dve", lambda e, pu=pu: e.tensor_copy(out=uv[0:64, :, :], in_=self.ps[pu][0:64, :].rearrange("p (h w) -> p h w", w=64)),
                  reads=["ps%d" % pu], writes=[K("uvu")])
            yield
            py = bank()
            pS = bank()
            for hl in range(HQ):
                P.add("pe", lambda e, hl=hl, py=py: e.matmul(self.ps[py][:, hl * 64:(hl + 1) * 64], lhsT=ext(rarr, hl, 64), rhs=ss[:, hl, :],
                                                            start=True, stop=False),
                      reads=[K("rarr"), K("ss")], writes=["ps%d" % py])
                P.add("pe", lambda e, hl=hl, py=py: e.matmul(self.ps[py][:, hl * 64:(hl + 1) * 64], lhsT=ext(mb, hl, 64), rhs=uv[:, hl, :],
                                                            start=False, stop=True),
                      reads=[K("mb"), K("uvu"), K("uvv")], writes=["ps%d" % py])
            for hl in range(HQ):
                P.add("pe", lambda e, hl=hl, pS=pS: e.matmul(self.ps[pS][:, hl * 64:(hl + 1) * 64], lhsT=IDr, rhs=ss[:, hl, :],
                                                            start=True, stop=False),
                      reads=["idr", K("ss")], writes=["ps%d" % pS])
                P.add("pe", lambda e, hl=hl, pS=pS: e.matmul(self.ps[pS][:, hl * 64:(hl + 1) * 64], lhsT=ext(lbkt, hl, 0), rhs=uv[:, hl, :],
                                                            start=False, stop=True),
                      reads=[K("lbkt"), K("uvu"), K("uvv")], writes=["ps%d" % pS])
            if ch >= TC // CH:
                yt = tl["yt"][b]
                P.add("act", lambda e, py=py, yt=yt: e.copy(out=yt, in_=self.ps[py][0:64, :].rearrange("p (h w) -> p h w", w=64)),
                      reads=["ps%d" % py], writes=[K("yt%d" % b)])
                tx = (ch - TC // CH) * CH
                P.add("sp", lambda e, yt=yt, tx=tx: e.dma_start(
                    out=self.YD[d][tx:tx + CH, hq * 512:(hq + 1) * 512].rearrange("t (h v) -> t h v", v=64), in_=yt),
                    reads=[K("yt%d" % b)], writes=[], dma=K("yst%d" % b))
            P.add("dve", lambda e, pS=pS, ch=ch: e.tensor_tensor(
                out=ss, in0=self.ps[pS][0:64, :].rearrange("p (h w) -> p h w", w=64),
                in1=tl["dc"][:, :, ch:ch + 1].to_broadcast([64, HQ, 64]), op=ALU.mult),
                reads=["ps%d" % pS, K("dc")], writes=[K("ss")])
            yield

    def rwkv_readout(self):
        P = self.P
        n = 512
        BG = self.BIGA[:, 0:24704].bitcast(F32)
        Y0 = BG[:, 0:2048]
        Y1 = BG[:, 2048:4096]
        VV = BG[:, 4096:6144]
        GG = BG[:, 6144:8192]
        OB = self.BIGA[:, 16384:18432]
        RW = self.RWB.bitcast(F32)
        LNW = RW[:, 0:2048]
        LNB = RW[:, 2048:4096]
        BON = RW[:, 4096:4128]
        MEAN = RW[:, 4128:4160]
        RSTD = RW[:, 4160:4192]
        h3 = lambda ap: ap.rearrange("p (h v) -> p h v", v=64)
        bc = lambda ap: ap.unsqueeze(2).to_broadcast([128, NH, 64])
        XTv = self.XT.rearrange("(c p) t -> p c t", p=128)
        xTv = self.xT.rearrange("(c p) t -> p c t", p=128)
        wo = self.rw_wo.rearrange("(kc p) n -> p kc n", p=128)
        P.add("sp", lambda e: e.dma_start(out=LNW, in_=self.lnwb[0]), writes=["lnw"], dma="lnw")
        P.add("sp", lambda e: e.dma_start(out=LNB, in_=self.lnwb[1]), writes=["lnb"], dma="lnb")
        for t in range(T // n):
            t0 = t * n
            P.add("sp", lambda e, t0=t0: e.dma_start(out=self.XTB, in_=xTv[:, :, t0:t0 + n]), writes=["xtb"], dma="xtb")
            for su in range(4):
                r0 = t0 + su * 128
                P.add("sp", lambda e, r0=r0: e.dma_start(out=Y0, in_=self.YD[0][r0:r0 + 128, :]), writes=["y0"], dma="y0")
                P.add("sp", lambda e, r0=r0: e.dma_start(out=Y1, in_=self.YD[1][r0:r0 + 128, :]), writes=["y1"], dma="y1")
                P.add("sp", lambda e, r0=r0: e.dma_start(out=VV, in_=self.VD[TC + r0:TC + r0 + 128, :]), writes=["vv"], dma="vv")
                P.add("sp", lambda e, r0=r0: e.dma_start(out=GG, in_=self.GD[r0:r0 + 128, :]), writes=["gg"], dma="gg")
                P.add("sp", lambda e, r0=r0: e.dma_start(out=BON, in_=self.BOND[r0:r0 + 128, :]), writes=["bon"], dma="bon")
                P.add("dve", lambda e: e.tensor_tensor(out=Y0, in0=Y0, in1=Y1, op=ALU.add), reads=["y0", "y1"], writes=["y0"])
                P.add("dve", lambda e: e.reduce_sum(out=MEAN, in_=h3(Y0), axis=AX.X), reads=["y0"], writes=["mean"])
                P.add("dve", lambda e: e.tensor_scalar(out=MEAN, in0=MEAN, scalar1=1.0 / 64, scalar2=None, op0=ALU.mult), reads=["mean"], writes=["mean"])
                P.add("dve", lambda e: e.tensor_tensor(out=h3(Y0), in0=h3(Y0), in1=bc(MEAN), op=ALU.subtract), reads=["y0", "mean"], writes=["y0"])
                P.add("act", lambda e: e.activation(out=Y1, in_=Y0, func=AF.Square), reads=["y0"], writes=["y1"])
                P.add("dve", lambda e: e.reduce_sum(out=RSTD, in_=h3(Y1), axis=AX.X), reads=["y1"], writes=["rstd"])
                P.add("act", lambda e: e.activation(out=RSTD, in_=RSTD, func=AF.Sqrt, scale=1.0 / 64, bias=GN_EPS), reads=["rstd"], writes=["rstd"])
                P.add("dve", lambda e: e.reciprocal(out=RSTD, in_=RSTD), reads=["rstd"], writes=["rstd"])
                P.add("dve", lambda e: e.tensor_tensor(out=h3(Y0), in0=h3(Y0), in1=bc(RSTD), op=ALU.mult), reads=["y0", "rstd"], writes=["y0"])
                P.add("dve", lambda e: e.tensor_tensor(out=Y0, in0=Y0, in1=LNW, op=ALU.mult), reads=["y0", "lnw"], writes=["y0"])
                P.add("dve", lambda e: e.tensor_tensor(out=Y0, in0=Y0, in1=LNB, op=ALU.add), reads=["y0", "lnb"], writes=["y0"])
                P.add("dve", lambda e: e.tensor_tensor(out=h3(VV), in0=h3(VV), in1=bc(BON), op=ALU.mult), reads=["vv", "bon"], writes=["vv"])
                P.add("dve", lambda e: e.tensor_tensor(out=Y0, in0=Y0, in1=VV, op=ALU.add), reads=["y0", "vv"], writes=["y0"])
                P.add("dve", lambda e: e.tensor_tensor(out=OB, in0=Y0, in1=GG, op=ALU.mult), reads=["y0", "gg"], writes=["ob"])
                for cg in range(4):
                    pi = self.next_ps(0, 6)
                    psb = self.ps[pi].bitcast(BF16)
                    for cc in range(4):
                        c = cg * 4 + cc
                        P.add("pe", lambda e, c=c, cc=cc, psb=psb: e.transpose(psb[:, cc * 128:(cc + 1) * 128], OB[:, c * 128:(c + 1) * 128], self.IDb),
                              reads=["ob", "idb"], writes=["ps%d" % pi])
                    P.add("act", lambda e, cg=cg, su=su, psb=psb: e.copy(
                        out=self.HB[:, cg * 4:(cg + 1) * 4, su * 128:(su + 1) * 128], in_=psb[:, 0:512].rearrange("p (c t) -> p c t", t=128)),
                        reads=["ps%d" % pi], writes=["hb"])
            self.proj_residual(wo, KC, lambda kc: self.HB[:, kc, :n], ["hb"], self.MS[0][:, 2, :], "ms0", n)
            P.add("sp", lambda e, t0=t0: e.dma_start(out=XTv[:, :, t0:t0 + n], in_=self.XTB),
                  reads=["xtb"], writes=["XT%d" % t], dma="xtbo")

    def layer1_mixer(self):
        P = self.P
        MS = self.MS[1]
        n = 512
        XTv = self.XT.rearrange("(c p) t -> p c t", p=128)
        ZDv = self.ZD.rearrange("(c p) t -> p c t", p=128)
        GBv = self.GBD.rearrange("(c p) t -> p c t", p=128)
        Z = self.BIGA[:, 0:16384].bitcast(F32).rearrange("p (c t) -> p c t", t=512)
        GB = self.BIGA[:, 16384:24576].rearrange("p (c t) -> p c t", t=512)
        win = self.sc_win.rearrange("(kc p) n -> p kc n", p=128)
        P.add("sp", lambda e: e.dma_start(out=ZDv[:, :, 0:1], in_=self.ZERO[:, 0:16].unsqueeze(2), allow_slow_non_contiguous=True), reads=["zero"], writes=["ZDpad"], dma="zd0")
        P.add("sp", lambda e: e.dma_start(out=ZDv[:, :, T + 1:T + 2], in_=self.ZERO[:, 0:16].unsqueeze(2), allow_slow_non_contiguous=True), reads=["zero"], writes=["ZDpad"], dma="zd0")
        for t in range(T // n):
            t0 = t * n
            P.add("sp", lambda e, t0=t0: e.dma_start(out=self.XTB, in_=XTv[:, :, t0:t0 + n]),
                  reads=["XT%d" % t], writes=["xtb"], dma="xtb")
            self.norm_mod(self.XTB, "xtb", MS[:, 0, :], MS[:, 1, :], "ms1", self.HB, "hb", n)
            for nb in range(4):
                slots = []
                for part in range(3):
                    slots.append(self.load_w(win[:, :, part * D + nb * 512: part * D + (nb + 1) * 512], 128, KC, 512))
                for jj in range(4):
                    j = nb * 4 + jj
                    pss = []
                    for part in range(3):
                        s, view = slots[part]
                        pi = self.next_ps(0, 6)
                        pss.append(pi)
                        for kc in range(KC):
                            P.add("pe", lambda e, view=view, pi=pi, kc=kc, jj=jj: e.matmul(
                                self.ps[pi][:, :n], lhsT=view[:, kc, jj * 128:(jj + 1) * 128], rhs=self.HB[:, kc, :n],
                                start=(kc == 0), stop=(kc == KC - 1)),
                                reads=["ws%d" % s, "hb"], writes=["ps%d" % pi])
                    q = self.tmp_i % 3
                    self.tmp_i += 1
                    TM = self.TMP[q]
                    P.add("act", lambda e, j=j, pi=pss[0]: e.copy(out=GB[:, j, :], in_=self.ps[pi][:, :n]),
                          reads=["ps%d" % pss[0]], writes=["bg1"])
                    P.add("act", lambda e, TM=TM, pi=pss[1]: e.copy(out=TM[:, :n], in_=self.ps[pi][:, :n]),
                          reads=["ps%d" % pss[1]], writes=["tmp%d" % q])
                    P.add("dve", lambda e, TM=TM, j=j, pi=pss[2]: e.tensor_tensor(
                        out=Z[:, j, :], in0=TM[:, :n], in1=self.ps[pi][:, :n], op=ALU.mult),
                        reads=["ps%d" % pss[2], "tmp%d" % q], writes=["bg0"])
            P.add("sp", lambda e, t0=t0: e.dma_start(out=ZDv[:, :, t0 + 1:t0 + 1 + n], in_=Z), reads=["bg0"], writes=["ZD%d" % t], dma="zst")
            P.add("sp", lambda e, t0=t0: e.dma_start(out=GBv[:, :, t0:t0 + n], in_=GB), reads=["bg1"], writes=["GBD%d" % t], dma="gbst")
        ZE = self.BIGA[:, 0:16 * 514 * 2].bitcast(F32).rearrange("p (c t) -> p c t", t=514)
        GB2 = self.BIGA[:, 16448:16448 + 8192].rearrange("p (c t) -> p c t", t=512)
        wout = self.sc_wout.rearrange("(kc p) n -> p kc n", p=128)
        for t in range(T // n):
            t0 = t * n
            P.add("sp", lambda e, t0=t0: e.dma_start(out=ZE, in_=ZDv[:, :, t0:t0 + n + 2]),
                  reads=["ZDpad"] + ["ZD%d" % u for u in (t - 1, t, t + 1) if 0 <= u < T // n], writes=["bg0", "bg1"], dma="zld")
            P.add("sp", lambda e, t0=t0: e.dma_start(out=GB2, in_=GBv[:, :, t0:t0 + n]),
                  reads=["GBD%d" % t], writes=["bg1"], dma="gbld")
            P.add("sp", lambda e, t0=t0: e.dma_start(out=self.XTB, in_=XTv[:, :, t0:t0 + n]),
                  reads=["XT%d" % t], writes=["xtb"], dma="xtb")
            for c in range(KC):
                q = self.tmp_i % 3
                self.tmp_i += 1
                TM = self.TMP[q]
                P.add("dve", lambda e, TM=TM, c=c: e.tensor_scalar(
                    out=TM[:, :n], in0=ZE[:, c, 0:n], scalar1=self.pvc("cw0", c), scalar2=None, op0=ALU.mult),
                    reads=["bg0", "bg1", "pv"], writes=["tmp%d" % q])
                P.add("dve", lambda e, TM=TM, c=c: e.scalar_tensor_tensor(
                    out=TM[:, :n], in0=ZE[:, c, 1:n + 1], scalar=self.pvc("cw1", c), in1=TM[:, :n], op0=ALU.mult, op1=ALU.add),
                    reads=["bg0", "bg1", "pv", "tmp%d" % q], writes=["tmp%d" % q])
                P.add("dve", lambda e, TM=TM, c=c: e.scalar_tensor_tensor(
                    out=TM[:, :n], in0=ZE[:, c, 2:n + 2], scalar=self.pvc("cw2", c), in1=TM[:, :n], op0=ALU.mult, op1=ALU.add),
                    reads=["bg0", "bg1", "pv", "tmp%d" % q], writes=["tmp%d" % q])
                P.add("dve", lambda e, TM=TM, c=c: e.tensor_tensor(
                    out=self.HB[:, c, :n], in0=TM[:, :n], in1=GB2[:, c, :], op=ALU.mult),
                    reads=["bg1", "tmp%d" % q], writes=["hb"])
            self.proj_residual(wout, KC, lambda kc: self.HB[:, kc, :n], ["hb"], MS[:, 2, :], "ms1", n)
            P.add("sp", lambda e, t0=t0: e.dma_start(out=XTv[:, :, t0:t0 + n], in_=self.XTB),
                  reads=["xtb"], writes=["XT%d" % t], dma="xtbo")

    def proj_residual(self, wsrc, nk, rhs_fn, rkeys, G, gkey, n):
        P = self.P
        wcols = 512 if nk <= 16 else 128
        for nb in range(D // wcols):
            s, view = self.load_w(wsrc[:, :, nb * wcols:(nb + 1) * wcols], 128, nk, wcols)
            for jj in range(wcols // 128):
                j = nb * (wcols // 128) + jj
                pi = self.next_ps(0, 6)
                for kc in range(nk):
                    P.add("pe", lambda e, view=view, pi=pi, kc=kc, jj=jj: e.matmul(
                        self.ps[pi][:, :n], lhsT=view[:, kc, jj * 128:(jj + 1) * 128], rhs=rhs_fn(kc),
                        start=(kc == 0), stop=(kc == nk - 1)),
                        reads=["ws%d" % s] + rkeys, writes=["ps%d" % pi])
                P.add("dve", lambda e, pi=pi, j=j: e.scalar_tensor_tensor(
                    out=self.XTB[:, j, :n], in0=self.ps[pi][:, :n], scalar=G[:, j:j + 1], in1=self.XTB[:, j, :n],
                    op0=ALU.mult, op1=ALU.add),
                    reads=["ps%d" % pi, gkey, "xtb"], writes=["xtb"])

    def ffn(self, i, final=False):
        P = self.P
        MS = self.MS[i]
        n = 512
        XTv = self.XT.rearrange("(c p) t -> p c t", p=128)
        OTv = self.outT.rearrange("(c p) t -> p c t", p=128)
        w13 = self.ffn_w13[i].rearrange("(kc p) n -> p kc n", p=128)
        w2 = self.ffn_w2[i].rearrange("(fc p) n -> p fc n", p=128)
        ACTT = self.BIGA[:, 0:NFC * 512].rearrange("p (f t) -> p f t", t=512)
        for t in range(T // n):
            t0 = t * n
            P.add("sp", lambda e, t0=t0: e.dma_start(out=self.XTB, in_=XTv[:, :, t0:t0 + n]),
                  reads=["XT%d" % t], writes=["xtb"], dma="xtb")
            self.norm_mod(self.XTB, "xtb", MS[:, 3, :], MS[:, 4, :], "ms%d" % i, self.HB, "hb", n)
            for fb in range(FF // 512):
                sa, va = self.load_w(w13[:, :, fb * 512:(fb + 1) * 512], 128, KC, 512)
                sb_, vb = self.load_w(w13[:, :, FF + fb * 512:FF + (fb + 1) * 512], 128, KC, 512)
                for jj in range(4):
                    f = fb * 4 + jj
                    pa = self.next_ps(0, 6)
                    pb = self.next_ps(0, 6)
                    for (pi, s, view) in ((pa, sa, va), (pb, sb_, vb)):
                        for kc in range(KC):
                            P.add("pe", lambda e, view=view, pi=pi, kc=kc, jj=jj: e.matmul(
                                self.ps[pi][:, :n], lhsT=view[:, kc, jj * 128:(jj + 1) * 128], rhs=self.HB[:, kc, :n],
                                start=(kc == 0), stop=(kc == KC - 1)),
                                reads=["ws%d" % s, "hb"], writes=["ps%d" % pi])
                    q = self.tmp_i % 3
                    self.tmp_i += 1
                    TM = self.TMP[q]
                    P.add("act", lambda e, TM=TM, pa=pa: e.activation(out=TM[:, :n], in_=self.ps[pa][:, :n], func=AF.Silu),
                          reads=["ps%d" % pa], writes=["tmp%d" % q])
                    P.add("dve", lambda e, TM=TM, pb=pb, f=f: e.tensor_tensor(
                        out=ACTT[:, f, :n], in0=TM[:, :n], in1=self.ps[pb][:, :n], op=ALU.mult),
                        reads=["ps%d" % pb, "tmp%d" % q], writes=["bg0", "bg1"])
            self.proj_residual(w2, NFC, lambda fc: ACTT[:, fc, :n], ["bg0", "bg1"], MS[:, 5, :], "ms%d" % i, n)
            if final:
                FO = self.BIGA[:, 0:16384].bitcast(F32).rearrange("p (c t) -> p c t", t=512)
                self.norm_mod(self.XTB, "xtb", self.pvc("fg"), None, "pv", FO, "bg0", n)
                P.add("sp", lambda e, t0=t0, FO=FO: e.dma_start(out=OTv[:, :, t0:t0 + n], in_=FO),
                      reads=["bg0"], writes=["outT%d" % t], dma="outst")
                if self.dbg:
                    P.add("sp", lambda e, t0=t0: e.dma_start(out=XTv[:, :, t0:t0 + n], in_=self.XTB),
                          reads=["xtb"], writes=["XT%d" % t], dma="xtbo")
            else:
                P.add("sp", lambda e, t0=t0: e.dma_start(out=XTv[:, :, t0:t0 + n], in_=self.XTB),
                      reads=["xtb"], writes=["XT%d" % t], dma="xtbo")


def make_in_maps(inp, cores):
    pv = pack_pvec(inp)
    shared = {
        "pvec": pv,
        "ada_w": np.ascontiguousarray(inp["ada_w"], np.float32),
        "sc_win": np.ascontiguousarray(inp["sc_win"][0], np.float32),
        "sc_wout": np.ascontiguousarray(inp["sc_wout"][0], np.float32),
        "ffn_w13": np.ascontiguousarray(inp["ffn_w13"], np.float32),
        "ffn_w2": np.ascontiguousarray(inp["ffn_w2"], np.float32),
        "rw_wr": np.ascontiguousarray(inp["rw_wr"][0], np.float32),
        "rw_wk": np.ascontiguousarray(inp["rw_wk"][0], np.float32),
        "rw_wv": np.ascontiguousarray(inp["rw_wv"][0], np.float32),
        "rw_wo": np.ascontiguousarray(inp["rw_wo"][0], np.float32),
        "rw_w1": np.ascontiguousarray(inp["rw_w1"][0], np.float32),
        "rw_w2": np.ascontiguousarray(inp["rw_w2"][0], np.float32),
        "rw_a1": np.ascontiguousarray(inp["rw_a1"][0], np.float32),
        "rw_a2": np.ascontiguousarray(inp["rw_a2"][0], np.float32),
        "rw_g1": np.ascontiguousarray(inp["rw_g1"][0], np.float32),
        "rw_g2": np.ascontiguousarray(inp["rw_g2"][0], np.float32),
        "lnwb": np.ascontiguousarray(np.stack([np.broadcast_to(inp["rw_lnw"][0], (128, D)),
                                               np.broadcast_to(inp["rw_lnb"][0], (128, D))]), np.float32),
        "cst": make_cst(),
    }
    maps = []
    for b in cores:
        m = dict(shared)
        m["xT"] = np.ascontiguousarray(np.asarray(inp["x"][b], np.float32).T)
        m["ctxT"] = np.ascontiguousarray(np.asarray(inp["ctx"][b], np.float32).T)
        cond = np.stack([_fm(inp["c"][b]), _fm(inp["c_ctx"])], axis=-1)
        m["cond"] = np.ascontiguousarray(cond, np.float32)
        maps.append(m)
    return maps


def kernel(**inputs):
    inp = {k: np.asarray(v) for k, v in inputs.items()}
    bld = Builder(start_layer=0)
    maps = make_in_maps(inp, list(range(8)))
    res = run_bass_kernel_spmd(bld.nc, maps, core_ids=list(range(8)))
    out = np.stack([np.ascontiguousarray(r["outT"].T) for r in res.results], axis=0)
    return out.astype(np.float32)
```
